# Optimizing a Trainium2 kernel written in Bass

```python
import math
import jax, jax.numpy as jnp
from jax import lax
import numpy as np


D_MODEL = 1024
BATCH = 8
SEQ = 4096
DEPTH = 1

N_META = 16
S5_WIDTH = D_MODEL // 2
S5_GROUP = 16
S5_GROUPS = S5_WIDTH // S5_GROUP
S5_STATE = 64
N_HEADS = D_MODEL // 128
HEAD_DIM = 64
V_DIM = 2 * HEAD_DIM
ATTN_WIDTH = N_HEADS * V_DIM
QK_WIDTH = N_HEADS * 2 * HEAD_DIM
REL_BUCKETS = 32
REL_MAX_DIST = 128
Q_BLOCK = 128
D_FF = 4 * D_MODEL
EPS = 1e-6
OFF_U = 0
OFF_Q = OFF_U + S5_WIDTH
OFF_K = OFF_Q + QK_WIDTH
OFF_V = OFF_K + QK_WIDTH
OFF_GS = OFF_V + ATTN_WIDTH
OFF_GA = OFF_GS + D_MODEL
IN_COLS = OFF_GA + D_MODEL

kernel_name = "hybrid_s5_diffattn_gated_encoder"


def _rmsnorm(x, g):
    xf = x.astype(jnp.float32)
    xf = xf * lax.rsqrt(jnp.mean(xf * xf, axis=-1, keepdims=True) + EPS)
    return xf.astype(x.dtype) * g


def _rel_bucket(rel):
    nb = REL_BUCKETS // 2
    ret = (rel > 0).astype(jnp.int32) * nb
    n = jnp.abs(rel)
    max_exact = nb // 2
    nf = jnp.maximum(n, 1).astype(jnp.float32)
    large = max_exact + (jnp.log(nf / max_exact) / math.log(REL_MAX_DIST / max_exact)
                         * (nb - max_exact)).astype(jnp.int32)
    large = jnp.minimum(large, nb - 1)
    return ret + jnp.where(n < max_exact, n, large)


def _complex_linear_combine(e1, e2):
    a1r, a1i, b1r, b1i = e1
    a2r, a2i, b2r, b2i = e2
    ar = a2r * a1r - a2i * a1i
    ai = a2r * a1i + a2i * a1r
    br = a2r * b1r - a2i * b1i + b2r
    bi = a2r * b1i + a2i * b1r + b2i
    return (ar, ai, br, bi)


def _s5_mixer(u, lam_re, lam_im, log_step, b_re, b_im, c_re, c_im, d_skip):
    bsz, L, _ = u.shape
    ug = u.reshape(bsz, L, S5_GROUPS, S5_GROUP)
    outs = []
    for dr in range(2):
        dt = jnp.exp(log_step[dr])[:, None]
        lr, li = lam_re[dr], lam_im[dr]
        mag = jnp.exp(lr * dt)
        a_re = mag * jnp.cos(li * dt)
        a_im = mag * jnp.sin(li * dt)
        den = lr * lr + li * li
        coef_re = ((a_re - 1.0) * lr + a_im * li) / den
        coef_im = (a_im * lr - (a_re - 1.0) * li) / den
        bu_re = jnp.einsum('blgc,gpc->lbgp', ug, b_re[dr])
        bu_im = jnp.einsum('blgc,gpc->lbgp', ug, b_im[dr])
        xr = coef_re * bu_re - coef_im * bu_im
        xi = coef_re * bu_im + coef_im * bu_re
        if dr == 1:
            xr, xi = xr[::-1], xi[::-1]
        ar = jnp.broadcast_to(a_re[None, None], (L, 1, S5_GROUPS, S5_STATE))
        ai = jnp.broadcast_to(a_im[None, None], (L, 1, S5_GROUPS, S5_STATE))
        _, _, sr, si = lax.associative_scan(_complex_linear_combine, (ar, ai, xr, xi), axis=0)
        if dr == 1:
            sr, si = sr[::-1], si[::-1]
        y_dir = (jnp.einsum('lbgp,gcp->blgc', sr, c_re[dr])
                 - jnp.einsum('lbgp,gcp->blgc', si, c_im[dr]))
        outs.append(y_dir)
    y = (outs[0] + outs[1]).reshape(bsz, L, S5_WIDTH)
    return y + d_skip * u


def _diff_attention(q, k, v, rel_table, lam):
    bsz, L = q.shape[0], q.shape[1]
    n_blk = -(-L // Q_BLOCK)
    Lq = n_blk * Q_BLOCK
    qp = jnp.pad(q, ((0, 0), (0, Lq - L), (0, 0), (0, 0), (0, 0)))
    q_blocks = qp.reshape(bsz, n_blk, Q_BLOCK, N_HEADS, 2, HEAD_DIM).transpose(1, 0, 2, 3, 4, 5)
    pos_blocks = jnp.arange(Lq, dtype=jnp.int32).reshape(n_blk, Q_BLOCK)
    k_pos = jnp.arange(L, dtype=jnp.int32)
    scale = HEAD_DIM ** -0.5
    lam32 = lam.astype(jnp.float32)

    def one_block(args):
        qb, qpos = args
        bucket = _rel_bucket(k_pos[None, :] - qpos[:, None])
        bias = jnp.transpose(rel_table[bucket], (2, 0, 1)).astype(jnp.float32)
        s = jnp.einsum('bqhtd,bkhtd->bhtqk', qb, k,
                       preferred_element_type=jnp.float32) * scale + bias[None, :, None]
        p = jax.nn.softmax(s, axis=-1)
        w = p[:, :, 0] - lam32 * p[:, :, 1]
        return jnp.einsum('bhqk,bkhe->bqhe', w.astype(v.dtype), v)

    o = lax.map(one_block, (q_blocks, pos_blocks))
    o = o.transpose(1, 0, 2, 3, 4).reshape(bsz, Lq, N_HEADS, V_DIM)[:, :L]
    return o


def setup_inputs(seed: int = 0) -> dict:
    key = jax.random.key(seed)
    ks = jax.random.split(key, 26)
    f32 = jnp.float32

    def nrm(k, shape, scale):
        return scale * jax.random.normal(k, shape, f32)

    n_idx = jnp.arange(S5_STATE, dtype=f32)
    sp = (DEPTH, 2, S5_GROUPS, S5_STATE)
    return {
        "x": nrm(ks[0], (BATCH, SEQ, D_MODEL), 1.0),
        "meta_tokens": nrm(ks[1], (N_META, D_MODEL), 1.0),
        "rel_bias_table": nrm(ks[2], (REL_BUCKETS, N_HEADS), 0.5),
        "norm_mix": 1.0 + nrm(ks[3], (DEPTH, D_MODEL), 0.02),
        "w_in": nrm(ks[4], (DEPTH, D_MODEL, IN_COLS), D_MODEL ** -0.5),
        "s5_lambda_re": -0.5 + nrm(ks[5], sp, 0.01),
        "s5_lambda_im": math.pi * n_idx + nrm(ks[6], sp, 0.01),
        "s5_log_step": jax.random.uniform(ks[7], (DEPTH, 2, S5_GROUPS), f32,
                                          minval=math.log(1e-3), maxval=math.log(1e-1)),
        "s5_b_re": nrm(ks[8], (DEPTH, 2, S5_GROUPS, S5_STATE, S5_GROUP), (2 * S5_GROUP) ** -0.5),
        "s5_b_im": nrm(ks[9], (DEPTH, 2, S5_GROUPS, S5_STATE, S5_GROUP), (2 * S5_GROUP) ** -0.5),
        "s5_c_re": nrm(ks[10], (DEPTH, 2, S5_GROUPS, S5_GROUP, S5_STATE), S5_STATE ** -0.5),
        "s5_c_im": nrm(ks[11], (DEPTH, 2, S5_GROUPS, S5_GROUP, S5_STATE), S5_STATE ** -0.5),
        "s5_d": nrm(ks[12], (DEPTH, S5_WIDTH), 1.0),
        "w_glu_a": nrm(ks[13], (DEPTH, S5_WIDTH, D_MODEL), S5_WIDTH ** -0.5),
        "w_glu_b": nrm(ks[14], (DEPTH, S5_WIDTH, D_MODEL), S5_WIDTH ** -0.5),
        "q_norm": 1.0 + nrm(ks[15], (DEPTH, HEAD_DIM), 0.02),
        "k_norm": 1.0 + nrm(ks[16], (DEPTH, HEAD_DIM), 0.02),
        "lambda_q1": nrm(ks[17], (DEPTH, HEAD_DIM), 0.1),
        "lambda_k1": nrm(ks[18], (DEPTH, HEAD_DIM), 0.1),
        "lambda_q2": nrm(ks[19], (DEPTH, HEAD_DIM), 0.1),
        "lambda_k2": nrm(ks[20], (DEPTH, HEAD_DIM), 0.1),
        "attn_subln": 1.0 + nrm(ks[21], (DEPTH, V_DIM), 0.02),
        "w_attn_out": nrm(ks[22], (DEPTH, ATTN_WIDTH, D_MODEL), ATTN_WIDTH ** -0.5),
        "w_o": nrm(ks[23], (DEPTH, D_MODEL, D_MODEL), D_MODEL ** -0.5),
        "norm_ff": 1.0 + nrm(ks[24], (DEPTH, D_MODEL), 0.02),
        "w_ff1": nrm(ks[25], (DEPTH, D_MODEL, D_FF), D_MODEL ** -0.5),
        "w_ff2": nrm(jax.random.fold_in(ks[25], 1), (DEPTH, D_FF, D_MODEL), D_FF ** -0.5),
    }


def reference(x, meta_tokens, rel_bias_table, norm_mix, w_in, s5_lambda_re, s5_lambda_im,
              s5_log_step, s5_b_re, s5_b_im, s5_c_re, s5_c_im, s5_d, w_glu_a, w_glu_b,
              q_norm, k_norm, lambda_q1, lambda_k1, lambda_q2, lambda_k2, attn_subln,
              w_attn_out, w_o, norm_ff, w_ff1, w_ff2):
    bsz = x.shape[0]
    meta = jnp.broadcast_to(meta_tokens[None].astype(x.dtype), (bsz, N_META, D_MODEL))
    h = jnp.concatenate([meta, x], axis=1)
    L = h.shape[1]
    for layer in range(DEPTH):
        lam_init = 0.8 - 0.6 * math.exp(-0.3 * layer)
        hn = _rmsnorm(h, norm_mix[layer])
        proj = hn @ w_in[layer]
        u = proj[..., OFF_U:OFF_Q]
        q = proj[..., OFF_Q:OFF_K].reshape(bsz, L, N_HEADS, 2, HEAD_DIM)
        k = proj[..., OFF_K:OFF_V].reshape(bsz, L, N_HEADS, 2, HEAD_DIM)
        v = proj[..., OFF_V:OFF_GS].reshape(bsz, L, N_HEADS, V_DIM)
        g_s5 = jax.nn.sigmoid(proj[..., OFF_GS:OFF_GA])
        g_attn = jax.nn.sigmoid(proj[..., OFF_GA:IN_COLS])
        y = _s5_mixer(u, s5_lambda_re[layer], s5_lambda_im[layer], s5_log_step[layer],
                      s5_b_re[layer], s5_b_im[layer], s5_c_re[layer], s5_c_im[layer],
                      s5_d[layer])
        y = jax.nn.gelu(y)
        y_s5 = (y @ w_glu_a[layer]) * jax.nn.sigmoid(y @ w_glu_b[layer])
        q = _rmsnorm(q, q_norm[layer])
        k = _rmsnorm(k, k_norm[layer])
        lam = (jnp.exp(jnp.sum(lambda_q1[layer] * lambda_k1[layer]))
               - jnp.exp(jnp.sum(lambda_q2[layer] * lambda_k2[layer])) + lam_init)
        o = _diff_attention(q, k, v, rel_bias_table, lam)
        o = _rmsnorm(o, attn_subln[layer]) * (1.0 - lam_init)
        y_attn = o.reshape(bsz, L, ATTN_WIDTH) @ w_attn_out[layer]
        merged = g_s5 * y_s5 + g_attn * y_attn
        h = h + merged @ w_o[layer]
        hn = _rmsnorm(h, norm_ff[layer])
        h = h + jnp.square(jax.nn.relu(hn @ w_ff1[layer])) @ w_ff2[layer]
    return h[:, N_META:]
```

```python
import numpy as np
import concourse.bass as bass
import concourse.mybir as mybir
from concourse.bass_utils import run_bass_kernel_spmd
from contextlib import ExitStack

F32 = mybir.dt.float32
BF16 = mybir.dt.bfloat16
I32 = mybir.dt.int32
AF = mybir.ActivationFunctionType
ALU = mybir.AluOpType
AX = mybir.AxisListType


class Prog:
    ENG = {"pe": "tensor", "act": "scalar", "dve": "vector", "pool": "gpsimd", "sp": "sync"}

    def __init__(self, nc, es, n_slots=12):
        self.nc = nc
        self.es = es
        self.ops = []
        self.last_w = {}
        self.readers = {}
        self.n_slots = n_slots
        self.fence_id = None

    def sb(self, name, shape, dtype):
        self.n_tensors = getattr(self, "n_tensors", 0) + 1
        return self.es.enter_context(self.nc.sbuf_tensor("%s_%d" % (name, self.n_tensors), list(shape), dtype))

    def ps(self, name, shape, dtype=F32):
        return self.es.enter_context(self.nc.psum_tensor(name, list(shape), dtype))

    def _add(self, eng, fn, r, w, dma):
        oid = len(self.ops)
        deps = {}
        for k in r:
            if k in self.last_w:
                deps[self.last_w[k]] = "hard"
        for k in w:
            if k in self.last_w:
                deps[self.last_w[k]] = "hard"
            for rd in self.readers.get(k, ()):
                if rd not in deps:
                    deps[rd] = "war"
        for k in r:
            self.readers.setdefault(k, []).append(oid)
        for k in w:
            self.last_w[k] = oid
            self.readers[k] = []
        if self.fence_id is not None and self.fence_id not in deps:
            deps[self.fence_id] = "hard"
        self.ops.append(dict(eng=eng, fn=fn, deps=deps, dma=dma))
        return oid

    def fence(self):
        nc = self.nc
        keys = list(set(self.last_w.keys()) | set(self.readers.keys()))
        self.fence_id = None
        fid = self._add("sp", lambda: nc.sync.nop(), [], keys, False)
        self.fence_id = fid
        return fid

    def op(self, eng, fn, r=(), w=()):
        return self._add(eng, fn, list(r), list(w), False)

    def dma(self, q, out, in_, r=(), w=()):
        nc = self.nc
        e = getattr(nc, self.ENG[q])
        return self._add(q, lambda: e.dma_start(out=out, in_=in_), list(r), list(w), True)

    def make_identity(self, ident, key, ones=None):
        nc = self.nc
        self.op("pool", lambda: nc.gpsimd.memset(ident[:], 1.0), w=[key])
        n = ident.shape[1]
        self.op("pool", lambda: nc.gpsimd.affine_select(
            out=ident[:], in_=ident[:], pattern=[[1, n]], compare_op=ALU.is_equal,
            fill=0.0, base=0, channel_multiplier=-1), r=[key], w=[key])

    def finish(self, out_keys):
        nc = self.nc
        self._add("sp", None, list(out_keys), [], False)
        ops = self.ops
        for o in ops:
            nd = {}
            for d, typ in o["deps"].items():
                p = ops[d]
                if not p["dma"] and not o["dma"] and p["eng"] == o["eng"]:
                    if o["eng"] == "pe":
                        continue
                nd[d] = typ
            o["deps"] = nd
        signalling = set()
        for o in ops:
            for d in o["deps"]:
                signalling.add(d)
        engs = ["pe", "act", "dve", "pool", "sp"]
        esem = {e: self.es.enter_context(nc.semaphore("s_" + e)) for e in engs}
        dsem = {}
        for q in ["sp", "pool", "act"]:
            if any(o["dma"] and o["eng"] == q for o in ops):
                dsem[q] = [self.es.enter_context(nc.semaphore("d_%s%d" % (q, i))) for i in range(self.n_slots)]
        ecount = {e: 0 for e in engs}
        dcount = {q: 0 for q in dsem}
        slot_val = {}
        slot_last = {}
        for i, o in enumerate(ops):
            if o["dma"]:
                q = o["eng"]
                s = dcount[q] % self.n_slots
                dcount[q] += 1
                v = slot_val.get((q, s), 0) + 16
                slot_val[(q, s)] = v
                o["sig"] = (("d", q, s), dsem[q][s], v)
                o["slot_prev"] = slot_last.get((q, s))
                slot_last[(q, s)] = i
            elif i in signalling:
                e = o["eng"]
                ecount[e] += 1
                o["sig"] = (("e", e), esem[e], ecount[e])
            else:
                o["sig"] = None
        seen = {}
        n_wait = 0
        for i, o in enumerate(ops):
            e = o["eng"]
            eobj = getattr(nc, self.ENG[e])
            deps = list(o["deps"].keys())
            if o["dma"] and o["slot_prev"] is not None:
                deps.append(o["slot_prev"])
            need = {}
            for d in deps:
                key, sem, val = ops[d]["sig"]
                if seen.get((e, key), 0) < val:
                    if key not in need or need[key][1] < val:
                        need[key] = (sem, val)
            for key, (sem, val) in need.items():
                eobj.wait_ge(sem, val)
                seen[(e, key)] = val
                n_wait += 1
            if o["fn"] is not None:
                ins = o["fn"]()
                if o["sig"] is not None:
                    ins.then_inc(o["sig"][1], 16 if o["dma"] else 1)
        self.stats = dict(n_ops=len(ops), n_wait=n_wait, ecount=ecount, dcount=dcount)
        return self.stats


NX = 4096
L = 4112
D = 1024
EPS = 1e-6
JW = 831
GW = 704
GD = 352
LAM_INIT = 0.2


def _rel_bucket_host(rel):
    nb = 16
    ret = (rel > 0).astype(np.int64) * nb
    n = np.abs(rel)
    nf = np.maximum(n, 1).astype(np.float32)
    large = 8 + (np.log(nf / np.float32(8)) / np.float32(np.log(16.0)) * np.float32(8)).astype(np.int64)
    large = np.minimum(large, nb - 1)
    return ret + np.where(n < 8, n, large)


def _onehot_const():
    i = np.arange(JW)
    rel = GD + 127 - i
    b = _rel_bucket_host(rel)
    oh = np.zeros((32, JW), np.float32)
    oh[b, i] = 1.0
    return oh


def sl(start, n, step=1):
    return slice(start, start + (n - 1) * step + 1, step)


def build_program(stage=99, debug=False):
    nc = bass.Bass("TRN2", target_bir_lowering=False)

    def din(name, shape):
        return nc.dram_tensor(name, list(shape), F32, kind="ExternalInput").ap()

    x = din("x", [NX, D])
    meta = din("meta", [16, D])
    relb = din("relb", [32, 8])
    oh = din("oh", [32, JW])
    nmix = din("nmix", [1, D])
    w_in = din("w_in", [D, 5632])
    lamre_d = din("lamre", [2, 32, 64])
    lamim_d = din("lamim", [2, 32, 64])
    lstep_d = din("lstep", [1, 64])
    bre_d = din("bre", [2, 32, 64, 16])
    bim_d = din("bim", [2, 32, 64, 16])
    cre_d = din("cre", [1024, 64])
    cim_d = din("cim", [1024, 64])
    s5d_d = din("s5d", [1, 512])
    w_a = din("w_a", [512, D])
    w_b = din("w_b", [512, D])
    qn_d = din("qn", [1, 64])
    kn_d = din("kn", [1, 64])
    lq1 = din("lq1", [1, 64])
    lk1 = din("lk1", [1, 64])
    lq2 = din("lq2", [1, 64])
    lk2 = din("lk2", [1, 64])
    subln_d = din("subln", [1, 128])
    w_ao = din("w_ao", [D, D])
    w_o = din("w_o", [D, D])
    nff = din("nff", [1, D])
    w_f1 = din("w_f1", [D, 4096])
    w_f2 = din("w_f2", [4096, D])
    out = nc.dram_tensor("out", [NX, D], F32, kind="ExternalOutput").ap()

    def dscr(name, shape, dt):
        return nc.dram_tensor(name, list(shape), dt).ap()

    wb_in = dscr("wb_in", [D, 5632], BF16)
    wb_a = dscr("wb_a", [512, D], BF16)
    wb_b = dscr("wb_b", [512, D], BF16)
    wb_ao = dscr("wb_ao", [D, D], BF16)
    wb_o = dscr("wb_o", [D, D], BF16)
    wb_f1 = dscr("wb_f1", [D, 4096], BF16)
    wb_f2 = dscr("wb_f2", [4096, D], BF16)
    fd = dscr("fd", [8, JW], BF16)
    ygd = dscr("ygd", [L, 512], BF16)
    hnTd = dscr("hnTd", [128, 8, L], BF16)
    dbg = {}

    es = ExitStack()
    P = Prog(nc, es, n_slots=12)

    def MM(o, lhsT, rhs, start, stop, r, w, **kw):
        P.op("pe", lambda: nc.tensor.matmul(o, lhsT=lhsT, rhs=rhs, start=start, stop=stop, **kw), r=r, w=w)

    def TR(o, in_, idn, r, w):
        P.op("pe", lambda: nc.tensor.transpose(out=o, in_=in_, identity=idn), r=r, w=w)

    def ACT(o, in_, func, r, w, **kw):
        P.op("act", lambda: nc.scalar.activation(out=o, in_=in_, func=func, **kw), r=r, w=w)

    def veng(e):
        return nc.vector if e == "dve" else nc.gpsimd

    def TT(e, o, a, b, op, r, w):
        P.op(e, lambda: veng(e).tensor_tensor(out=o, in0=a, in1=b, op=op), r=r, w=w)

    def TS(e, o, a, s1, s2, op0, op1, r, w):
        if op1 is None:
            P.op(e, lambda: veng(e).tensor_scalar(out=o, in0=a, scalar1=s1, scalar2=None, op0=op0), r=r, w=w)
        else:
            P.op(e, lambda: veng(e).tensor_scalar(out=o, in0=a, scalar1=s1, scalar2=s2, op0=op0, op1=op1), r=r, w=w)

    def STT(o, a, s, b, op0, op1, r, w):
        P.op("dve", lambda: nc.vector.scalar_tensor_tensor(out=o, in0=a, scalar=s, in1=b, op0=op0, op1=op1), r=r, w=w)

    def CP(e, o, in_, r, w):
        if e == "act":
            P.op("act", lambda: nc.scalar.copy(out=o, in_=in_), r=r, w=w)
        else:
            P.op(e, lambda: veng(e).tensor_copy(out=o, in_=in_), r=r, w=w)

    def RECIP(o, in_, r, w):
        P.op("dve", lambda: nc.vector.reciprocal(out=o, in_=in_), r=r, w=w)

    def MEMSET(e, ap, val, w):
        P.op(e, lambda: veng(e).memset(ap, val), r=[], w=w)

    pb = [P.ps("pb%d" % i, [128, 512], F32) for i in range(8)]
    pbb = [pb[i][:].bitcast(BF16) for i in range(8)]
    PK = ["pb%d" % i for i in range(8)]

    ident = P.sb("ident", [128, 128], BF16)
    identf = P.sb("identf", [128, 128], F32)
    bones = P.sb("bones", [128, 128], BF16)
    mhalf = P.sb("mhalf", [128, 1], F32)
    epsb = P.sb("epsb", [128, 1], F32)
    MEMSET("pool", epsb[:], EPS, ["epsb"])
    MEMSET("pool", mhalf[:], -0.5, ["mhalf"])
    P.make_identity(ident, "ident")
    P.make_identity(identf, "identf")
    MEMSET("pool", bones[:], 0.0, ["bones"])
    MEMSET("pool", bones[0:64, 0:64], 1.0 / 64, ["bones"])
    MEMSET("pool", bones[64:128, 64:128], 1.0 / 64, ["bones"])

    gmix = P.sb("gmix", [128, 8], F32)
    gff = P.sb("gff", [128, 8], F32)
    gsub = P.sb("gsub", [128, 1], F32)
    qg = P.sb("qg", [128, 1], F32)
    kg = P.sb("kg", [128, 1], F32)
    lamt = P.sb("lamt", [128, 1], F32)
    farb = P.sb("farb", [128, 2, 8], F32)
    with nc.allow_non_contiguous_dma(reason="small param loads"):
        pass
    P.dma("sp", gmix[:], nmix.rearrange("o (k p) -> p (o k)", p=128), w=["gmix"])
    P.dma("sp", gff[:], nff.rearrange("o (k p) -> p (o k)", p=128), w=["gff"])
    P.dma("sp", gsub[:], subln_d.rearrange("o p -> p o"), w=["gsub"])
    TS("dve", gsub[:], gsub[:], 1.0 - LAM_INIT, None, ALU.mult, None, ["gsub"], ["gsub"])
    for hf in range(2):
        P.dma("sp", qg[64 * hf:64 * hf + 64, :], qn_d.rearrange("o p -> p o"), w=["qg"])
        P.dma("sp", kg[64 * hf:64 * hf + 64, :], kn_d.rearrange("o p -> p o"), w=["kg"])
    P.dma("sp", farb[:, 0, :], relb[15:16, :].broadcast_to((128, 8)), w=["farb"])
    P.dma("sp", farb[:, 1, :], relb[31:32, :].broadcast_to((128, 8)), w=["farb"])
    TT("dve", farb[:, 1, :], farb[:, 1, :], farb[:, 0, :], ALU.subtract, ["farb"], ["farb"])
    with ExitStack() as es0:
        P.es = es0
        l4 = P.sb("l4", [128, 4, 64], F32)
        lp = P.sb("lp", [128, 2, 64], F32)
        ls = P.sb("ls", [128, 2], F32)
        for i, a in enumerate([lq1, lk1, lq2, lk2]):
            P.dma("sp", l4[:, i, :], a.broadcast_to((128, 64)), w=["l4"])
        TT("dve", lp[:, 0, :], l4[:, 0, :], l4[:, 1, :], ALU.mult, ["l4"], ["lp"])
        TT("dve", lp[:, 1, :], l4[:, 2, :], l4[:, 3, :], ALU.mult, ["l4", "lp"], ["lp"])
        P.op("dve", lambda: nc.vector.reduce_sum(out=ls[:], in_=lp[:], axis=AX.X), r=["lp"], w=["ls"])
        ACT(ls[:], ls[:], AF.Exp, ["ls"], ["ls"])
        TT("dve", lamt[:], ls[:, 0:1], ls[:, 1:2], ALU.subtract, ["ls"], ["lamt"])
        TS("dve", lamt[:], lamt[:], LAM_INIT, None, ALU.add, None, ["lamt"], ["lamt"])
    P.fence()
    P.es = es
    if stage <= 0:
        return nc, P, es, dbg

    Gt = P.sb("Gt", [128, 8, GW], BF16)
    chunks1, chunks2 = [], []
    for (src, dst, K, N, gain) in [
        (w_in, wb_in, D, 5632, "mix"), (w_a, wb_a, 512, D, None), (w_b, wb_b, 512, D, None),
        (w_ao, wb_ao, D, D, "sub"), (w_o, wb_o, D, D, None),
        (w_f1, wb_f1, D, 4096, "ff"), (w_f2, wb_f2, 4096, D, None),
    ]:
        for kt in range(K // 128):
            for c0 in range(0, N, 1024):
                cw = min(1024, N - c0)
                g = None
                if gain == "mix":
                    g = (gmix[:, kt:kt + 1], "gmix")
                elif gain == "ff":
                    g = (gff[:, kt:kt + 1], "gff")
                elif gain == "sub":
                    g = (gsub[:, 0:1], "gsub")
                ch = (src[kt * 128:(kt + 1) * 128, c0:c0 + cw], dst[kt * 128:(kt + 1) * 128, c0:c0 + cw], cw, g, dst.tensor.name)
                (chunks1 if (src is w_in and c0 == 0) else chunks2).append(ch)
    Wc = {"chunks": chunks1, "i": 0, "loaded": -1, "wst": None, "wob": None, "NB": 3}

    def w_load(i):
        sc, d, cw, g, nm = Wc["chunks"][i]
        bi = i % Wc["NB"]
        P.dma("sp", Wc["wst"][bi][:, 0:cw], sc, w=["wst%d" % bi])
        Wc["loaded"] = i

    def w_step(engs):
        i = Wc["i"]
        if i >= len(Wc["chunks"]):
            return False
        if Wc["loaded"] < i:
            w_load(i)
        if i + 1 < len(Wc["chunks"]):
            w_load(i + 1)
        sc, d, cw, g, nm = Wc["chunks"][i]
        bi = i % Wc["NB"]
        e = engs[i % len(engs)]
        wst, wob = Wc["wst"], Wc["wob"]
        rk = ["wst%d" % bi] + ([g[1]] if g else [])
        if e == "act":
            if g:
                ACT(wob[bi][:, 0:cw], wst[bi][:, 0:cw], AF.Copy, rk, ["wob%d" % bi], scale=g[0])
            else:
                CP("act", wob[bi][:, 0:cw], wst[bi][:, 0:cw], rk, ["wob%d" % bi])
        else:
            if g:
                TS(e, wob[bi][:, 0:cw], wst[bi][:, 0:cw], g[0], None, ALU.mult, None, rk, ["wob%d" % bi])
            else:
                CP(e, wob[bi][:, 0:cw], wst[bi][:, 0:cw], rk, ["wob%d" % bi])
        P.dma("sp", d, wob[bi][:, 0:cw], r=["wob%d" % bi], w=[nm])
        Wc["i"] += 1
        return True

    with ExitStack() as esw:
        P.es = esw
        tab = P.sb("tab", [32, 8], F32)
        ohs = P.sb("ohs", [32, JW], F32)
        fsb = P.sb("fsb", [8, JW], BF16)
        cneg = P.sb("cneg", [8, 1], F32)
        P.dma("sp", tab[:], relb, w=["tab"])
        P.dma("sp", cneg[:], relb[15:16, :].rearrange("o h -> h o"), w=["cneg"])
        P.dma("sp", ohs[:], oh, w=["ohs"])
        for c0 in range(0, JW, 512):
            cw = min(512, JW - c0)
            MM(pb[0][0:8, 0:cw], tab[:], ohs[:, c0:c0 + cw], True, True, ["tab", "ohs"], [PK[0]])
            TS("dve", fsb[:, c0:c0 + cw], pb[0][0:8, 0:cw], cneg[:, 0:1], 8.0, ALU.subtract, ALU.mult, [PK[0], "cneg"], ["fsb"])
        P.dma("sp", fd, fsb[:], r=["fsb"], w=["fd"])
        Wc["wst"] = [P.sb("wst%d" % i, [128, 1024], F32) for i in range(3)]
        Wc["wob"] = [P.sb("wob%d" % i, [128, 1024], BF16) for i in range(3)]
        while w_step(["dve", "act"]):
            pass
    P.fence()
    P.es = es
    for k in range(128):
        src = bass.AP(fd.tensor, 127 - k, [[0, 1], [JW, 8], [1, GW]])
        P.dma("pool", Gt[k:k + 1, :, :], src, r=["fd"], w=["Gtrow%d" % k])
    P.op("pool", lambda: nc.gpsimd.nop(), r=["Gtrow%d" % k for k in range(128)], w=["Gt"])
    if stage <= 1:
        return nc, P, es, dbg

    hk = {"i": 0}

    def make_hnT(src_rows, npart, dst, dkey, xkeep=None):
        i = hk["i"]
        hk["i"] += 1
        b = i % 2
        xt = xts[b]
        xk = "xt%d" % b
        if xkeep is not None:
            xt, xk = xkeep
        P.dma("sp", xt[0:npart, :], src_rows, w=[xk])
        ACT(xn[b][0:npart, :], xt[0:npart, :], AF.Square, [xk], ["xn%d" % b, "xss%d" % b], accum_out=xss[b][0:npart, :])
        TS("dve", xss[b][0:npart, :], xss[b][0:npart, :], 1.0 / D, EPS, ALU.mult, ALU.add, ["xss%d" % b], ["xss%d" % b])
        ACT(xss[b][0:npart, :], xss[b][0:npart, :], AF.Sqrt, ["xss%d" % b], ["xss%d" % b])
        RECIP(xss[b][0:npart, :], xss[b][0:npart, :], ["xss%d" % b], ["xss%d" % b])
        TS("dve", xn[b][0:npart, :], xt[0:npart, :], xss[b][0:npart, 0:1], None, ALU.mult, None, [xk, "xss%d" % b], ["xn%d" % b])
        pbi = 6 + b
        pv = pbb[pbi].rearrange("p (k t) -> p k t", t=128)
        for k in range(8):
            TR(pv[:, k, 0:npart], xn[b][0:npart, k * 128:(k + 1) * 128], ident[0:npart, 0:npart], ["xn%d" % b, "ident"], [PK[pbi]])
        CP("dve" if b == 0 else "pool" if False else "dve", dst, pv[:, :, 0:npart], [PK[pbi]], [dkey])

    xts = [P.sb("xt%d" % i, [128, D], F32) for i in range(2)]
    xss = [P.sb("xss%d" % i, [128, 1], F32) for i in range(2)]
    xn = [P.sb("xn%d" % i, [128, D], BF16) for i in range(2)]

    PI = float(np.pi)
    es5 = ExitStack()
    P.es = es5
    U = P.sb("U", [128, 32, 2, 260], BF16)
    with ExitStack() as e1a:
        P.es = e1a
        hnTb = P.sb("hnTb", [128, 8, 2048], BF16)
        UU = P.sb("UU", [128, 32, 16, 16], BF16)
        wu = P.sb("wu", [128, 8, 512], BF16)
        P.dma("sp", wu[:], wb_in[:, 0:512].rearrange("(k p) n -> p k n", p=128), r=["wb_in"], w=["wu"])
        for sbi in range(3):
            if sbi < 2:
                for j in range(16):
                    r0 = 2048 * sbi + 128 * j
                    make_hnT(x[r0:r0 + 128, :], 128, hnTb[:, :, 128 * j:128 * j + 128], "hnTb")
                nrow = 128
                n0 = 1 + 128 * sbi
                P.dma("sp", hnTd[:, :, 16 + 2048 * sbi:16 + 2048 * (sbi + 1)], hnTb[:, :, 0:2048], r=["hnTb"], w=["hnTd%d" % sbi])
            else:
                make_hnT(meta[:, :], 16, hnTb[:, :, 0:16], "hnTb")
                nrow = 1
                n0 = 0
                P.dma("sp", hnTd[:, :, 0:16], hnTb[:, :, 0:16], r=["hnTb"], w=["hnTd2"])
            for tp in range(16):
                pbi = tp % 2
                for k in range(8):
                    MM(pb[pbi][0:nrow, :], hnTb[:, k, sl(tp, nrow, 16)], wu[:, k, :], k == 0, k == 7,
                       ["hnTb", "wu"], [PK[pbi]])
                CP("act" if tp % 2 else "dve", UU[0:nrow, :, tp, :],
                   pb[pbi][0:nrow, :].rearrange("p (g c) -> p g c", c=16), [PK[pbi]], ["UU"])
            for g0 in range(0, 32, 4):
                pbi = 2 + (g0 // 4) % 2
                pv = pbb[pbi].rearrange("p (a h n) -> p a h n", a=4, h=2)
                for a in range(4):
                    uflat = UU[0:nrow, g0 + a].rearrange("p t c -> p (t c)")
                    for hf in range(2):
                        TR(pv[:, a, hf, 0:nrow], uflat[:, 128 * hf:128 * hf + 128], ident[0:nrow, 0:nrow],
                           ["UU", "ident"], [PK[pbi]])
                CP("dve", U[:, g0:g0 + 4, :, n0:n0 + nrow], pv[:, :, :, 0:nrow], [PK[pbi]], ["U"])
    P.fence()
    P.es = es5
    if debug and stage == 2:
        dbg["U"] = nc.dram_tensor("dbg_U", [128, 32 * 2 * 260], BF16, kind="ExternalOutput").ap()
        P.dma("sp", dbg["U"], U[:].rearrange("p g h n -> p (g h n)"), r=["U"], w=["dbgU"])
        return nc, P, es, dbg

    SPW = {}
    for nm in ["PWA", "PWB", "PWZ", "PWY", "Bt", "Cc"]:
        SPW[nm] = (P.sb("S" + nm + "r", [128, 32, 16], BF16), P.sb("S" + nm + "i", [128, 32, 16], BF16))
    AR2 = P.sb("AR2", [128, 2, 32], F32)
    AI2 = P.sb("AI2", [128, 2, 32], F32)
    drep = P.sb("drep", [128, 32], F32)
    maskf = P.sb("maskf", [128, 2, 256], F32)
    maskb = P.sb("maskb", [128, 2, 256], F32)
    for j in range(8):
        P.dma("sp", drep[16 * j:16 * j + 16, :], s5d_d[0, :].rearrange("(g c) -> c g", c=16), w=["drep"])
    with ExitStack() as ept:
        P.es = ept

        def T(name, shape=(128, 64), dt=F32):
            return P.sb("t_" + name, list(shape), dt)

        PW = {}
        for nm in ["PWA", "PWB", "PWZ", "PWY"]:
            PW[nm] = (P.sb(nm + "r", [128, 64, 16], BF16), P.sb(nm + "i", [128, 64, 16], BF16))
        Btr = P.sb("Btr", [128, 64, 16], BF16)
        Bti = P.sb("Bti", [128, 64, 16], BF16)
        Ccr = P.sb("Ccr", [128, 64, 16], BF16)
        Cci = P.sb("Cci", [128, 64, 16], BF16)

        lre, lim, lst = T("lre"), T("lim"), T("lst")
        bre = T("bre", (128, 64, 16))
        bim = T("bim", (128, 64, 16))
        crow = [T("crow0", (128, 8, 2, 64)), T("crow1", (128, 8, 2, 64))]
        cre = T("cre", (128, 64, 16))
        cim = T("cim", (128, 64, 16))
        for hf in range(2):
            s_ = slice(64 * hf, 64 * hf + 64)
            P.dma("sp", lre[s_], lamre_d.rearrange("d g p -> p (d g)"), w=["lre"])
            P.dma("sp", lim[s_], lamim_d.rearrange("d g p -> p (d g)"), w=["lim"])
            P.dma("sp", lst[s_], lstep_d.broadcast_to((64, 64)), w=["lst"])
            P.dma("sp", bre[s_], bre_d.rearrange("d g p c -> p (d g) c"), w=["bre"])
            P.dma("sp", bim[s_], bim_d.rearrange("d g p c -> p (d g) c"), w=["bim"])
            P.dma("sp", crow[0][:, :, hf, :], cre_d.rearrange("(b r) p -> r b p", r=128), w=["crow0"])
            P.dma("sp", crow[1][:, :, hf, :], cim_d.rearrange("(b r) p -> r b p", r=128), w=["crow1"])
        for ci, cdst in enumerate([cre, cim]):
            for b in range(8):
                pbi = b % 2
                TR(pb[pbi][:, 0:128], crow[ci][:, b].rearrange("r u p -> r (u p)"), identf[:], ["crow%d" % ci, "identf"], [PK[pbi]])
                CP("dve", cdst[:, 8 * b:8 * b + 8, :], pb[pbi][:, 0:128].rearrange("p (a c) -> p a c", c=16), [PK[pbi]], ["c%d" % ci])
        CP("dve", Ccr[:], cre[:], ["c0"], ["Ccr"])
        CP("dve", Cci[:], cim[:], ["c1"], ["Cci"])
        dtt, tmp, mag, inv, th, kf, rr, m1, sn, cs = [T(n) for n in ["dtt", "tmp", "mag", "inv", "th", "kf", "rr", "m1", "sn", "cs"]]
        ki = T("ki", (128, 64), I32)
        are, aim, den, am1, cfr, cfi, t1, t2 = [T(n) for n in ["are", "aim", "den", "am1", "cfr", "cfi", "t1", "t2"]]
        K_ = lambda *a: list(a)
        ACT(dtt[:], lst[:], AF.Exp, ["lst"], ["dtt"])
        TT("dve", tmp[:], lre[:], dtt[:], ALU.mult, ["lre", "dtt"], ["tmp"])
        ACT(mag[:], tmp[:], AF.Exp, ["tmp"], ["mag"])
        ACT(inv[:], tmp[:], AF.Exp, ["tmp"], ["inv"], scale=-2.0)
        TT("dve", th[:], lim[:], dtt[:], ALU.mult, ["lim", "dtt"], ["th"])
        TS("dve", kf[:], th[:], 1.0 / (2 * PI), 0.5, ALU.mult, ALU.add, ["th"], ["kf"])
        CP("dve", ki[:], kf[:], ["kf"], ["ki"])
        CP("dve", kf[:], ki[:], ["ki"], ["kf"])
        STT(rr[:], kf[:], -2 * PI, th[:], ALU.mult, ALU.add, ["kf", "th"], ["rr"])

        def wrap(v, key):
            TS("dve", m1[:], v[:], PI, None, ALU.is_gt, None, [key], ["m1"])
            STT(v[:], m1[:], -2 * PI, v[:], ALU.mult, ALU.add, ["m1", key], [key])
            TS("dve", m1[:], v[:], -PI, None, ALU.is_lt, None, [key], ["m1"])
            STT(v[:], m1[:], 2 * PI, v[:], ALU.mult, ALU.add, ["m1", key], [key])

        PIc = 3.1415925
        wrap(rr, "rr")
        TS("dve", rr[:], rr[:], PIc, -PIc, ALU.min, ALU.max, ["rr"], ["rr"])
        ACT(sn[:], rr[:], AF.Sin, ["rr"], ["sn"])
        TS("dve", rr[:], rr[:], PI / 2, None, ALU.add, None, ["rr"], ["rr"])
        wrap(rr, "rr")
        TS("dve", rr[:], rr[:], PIc, -PIc, ALU.min, ALU.max, ["rr"], ["rr"])
        ACT(cs[:], rr[:], AF.Sin, ["rr"], ["cs"])
        TT("dve", are[:], mag[:], cs[:], ALU.mult, ["mag", "cs"], ["are"])
        TT("dve", aim[:], mag[:], sn[:], ALU.mult, ["mag", "sn"], ["aim"])
        TT("dve", den[:], lre[:], lre[:], ALU.mult, ["lre"], ["den"])
        TT("dve", t1[:], lim[:], lim[:], ALU.mult, ["lim"], ["t1"])
        TT("dve", den[:], den[:], t1[:], ALU.add, ["den", "t1"], ["den"])
        RECIP(den[:], den[:], ["den"], ["den"])
        TS("dve", am1[:], are[:], -1.0, None, ALU.add, None, ["are"], ["am1"])
        TT("dve", t1[:], am1[:], lre[:], ALU.mult, ["am1", "lre"], ["t1"])
        TT("dve", t2[:], aim[:], lim[:], ALU.mult, ["aim", "lim"], ["t2"])
        TT("dve", t1[:], t1[:], t2[:], ALU.add, ["t1", "t2"], ["t1"])
        TT("dve", cfr[:], t1[:], den[:], ALU.mult, ["t1", "den"], ["cfr"])
        TT("dve", t1[:], aim[:], lre[:], ALU.mult, ["aim", "lre"], ["t1"])
        TT("dve", t2[:], am1[:], lim[:], ALU.mult, ["am1", "lim"], ["t2"])
        TT("dve", t1[:], t1[:], t2[:], ALU.subtract, ["t1", "t2"], ["t1"])
        TT("dve", cfi[:], t1[:], den[:], ALU.mult, ["t1", "den"], ["cfi"])
        w1 = T("w1", (128, 64, 17))
        w2 = T("w2", (128, 64, 17))

        def cmul(o_r, o_i, a_r, a_i, b_r, b_i, shp, rk, wk, neg_im=False):
            n = shp[-1]
            v1 = w1[:, :shp[1], 0:n] if len(shp) == 3 else w1[:, :, 0]
            v2 = w2[:, :shp[1], 0:n] if len(shp) == 3 else w2[:, :, 0]
            TT("dve", v1, a_r, b_r, ALU.mult, rk, ["w1"])
            TT("dve", v2, a_i, b_i, ALU.mult, rk, ["w2"])
            TT("dve", o_r, v1, v2, ALU.subtract, ["w1", "w2"], wk)
            TT("dve", v1, a_r, b_i, ALU.mult, rk, ["w1"])
            TT("dve", v2, a_i, b_r, ALU.mult, rk, ["w2"])
            TT("dve", o_i, v1, v2, ALU.add, ["w1", "w2"], wk)

        bc16 = lambda t_: t_[:, :, None].broadcast_to((128, 64, 16))
        cmul(Btr[:], Bti[:], bc16(cfr), bc16(cfi), bre[:], bim[:], (128, 64, 16), ["cfr", "cfi", "bre", "bim"], ["Bt"])
        posr, posi = T("posr", (128, 64, 17)), T("posi", (128, 64, 17))
        negr, negi = T("negr", (128, 64, 17)), T("negi", (128, 64, 17))
        air, aii = T("air"), T("aii")
        TT("dve", air[:], are[:], inv[:], ALU.mult, ["are", "inv"], ["air"])
        TT("dve", aii[:], aim[:], inv[:], ALU.mult, ["aim", "inv"], ["aii"])
        TS("dve", aii[:], aii[:], -1.0, None, ALU.mult, None, ["aii"], ["aii"])
        for (pr, pi_, ar_, ai_, kk, akeys) in [(posr, posi, are, aim, "pos", ["are", "aim"]), (negr, negi, air, aii, "neg", ["air", "aii"])]:
            MEMSET("dve", pr[:, :, 0:1], 1.0, [kk])
            MEMSET("dve", pi_[:, :, 0:1], 0.0, [kk])
            CP("dve", pr[:, :, 1], ar_[:], akeys + [kk], [kk])
            CP("dve", pi_[:, :, 1], ai_[:], akeys + [kk], [kk])
            m = 1
            while m < 16:
                shp = (128, 64, m)
                cmul(pr[:, :, m + 1:2 * m + 1], pi_[:, :, m + 1:2 * m + 1],
                     pr[:, :, 1:m + 1], pi_[:, :, 1:m + 1],
                     pr[:, :, m:m + 1].broadcast_to(shp), pi_[:, :, m:m + 1].broadcast_to(shp),
                     shp, [kk], [kk])
                m *= 2
        lo, hi = slice(0, 32), slice(32, 64)
        for ri, (ps_, ng_) in enumerate([(posr, negr), (posi, negi)]):
            CP("dve", PW["PWA"][ri][:, lo, :], ng_[:, lo, 0:16], ["neg"], ["PWA"])
            CP("dve", PW["PWA"][ri][:, hi, :], ps_[:, hi, 0:16], ["pos"], ["PWA"])
            CP("dve", PW["PWB"][ri][:, lo, :], ps_[:, lo, 0:16], ["pos"], ["PWB"])
            CP("dve", PW["PWB"][ri][:, hi, :], ng_[:, hi, 0:16], ["neg"], ["PWB"])
            CP("dve", PW["PWZ"][ri][:, hi, :], ps_[:, hi, 0:16], ["pos"], ["PWZ"])
            CP("dve", PW["PWY"][ri][:, lo, :], ps_[:, lo, 1:17], ["pos"], ["PWY"])
        shp = (128, 32, 16)
        cmul(PW["PWZ"][0][:, lo, :], PW["PWZ"][1][:, lo, :], negr[:, lo, 0:16], negi[:, lo, 0:16],
             posr[:, lo, 15:16].broadcast_to(shp), posi[:, lo, 15:16].broadcast_to(shp), shp, ["pos", "neg"], ["PWZ"])
        cmul(PW["PWY"][0][:, hi, :], PW["PWY"][1][:, hi, :], negr[:, hi, 0:16], negi[:, hi, 0:16],
             posr[:, hi, 16:17].broadcast_to(shp), posi[:, hi, 16:17].broadcast_to(shp), shp, ["pos", "neg"], ["PWY"])
        for ri in range(2):
            CP("dve", AR2[0:64, ri, :], posr[0:64, lo, 16], ["pos"], ["AR2"])
            CP("dve", AR2[64:128, ri, :], posr[64:128, hi, 16], ["pos"], ["AR2"])
        CP("dve", AI2[0:64, 1, :], posi[0:64, lo, 16], ["pos"], ["AI2"])
        CP("dve", AI2[64:128, 1, :], posi[64:128, hi, 16], ["pos"], ["AI2"])
        TS("dve", AI2[:, 0, :], AI2[:, 1, :], -1.0, None, ALU.mult, None, ["AI2"], ["AI2"])
        qi = T("qi", (128, 1), I32)
        qf = T("qf", (128, 2))
        li_ = T("li_", (128, 256), I32)
        lf_ = T("lf_", (128, 256))
        P.op("pool", lambda: nc.gpsimd.iota(qi[:], pattern=[[0, 1]], base=0, channel_multiplier=1), w=["qi"])
        P.op("dve", lambda: nc.vector.tensor_scalar(out=qi[:], in0=qi[:], scalar1=4, scalar2=None, op0=ALU.arith_shift_right), r=["qi"], w=["qi"])
        CP("dve", qf[:, 0:1], qi[:], ["qi"], ["qf"])
        TS("dve", qf[:, 1:2], qf[:, 0:1], 8.0, None, ALU.add, None, ["qf"], ["qf"])
        P.op("pool", lambda: nc.gpsimd.iota(li_[:], pattern=[[1, 16], [0, 16]], base=0, channel_multiplier=0), w=["li_"])
        CP("dve", lf_[:], li_[:], ["li_"], ["lf_"])
        for hf in range(2):
            TS("dve", maskf[:, hf, :], lf_[:], qf[:, hf:hf + 1], None, ALU.is_ge, None, ["lf_", "qf"], ["maskf"])
            TS("dve", maskb[:, hf, :], lf_[:], qf[:, hf:hf + 1], None, ALU.is_le, None, ["lf_", "qf"], ["maskb"])
        for nm, (sr, si_), rk in [("PWA", PW["PWA"], ["PWA"]), ("PWB", PW["PWB"], ["PWB"]), ("PWZ", PW["PWZ"], ["PWZ"]),
                                  ("PWY", PW["PWY"], ["PWY"]), ("Bt", (Btr, Bti), ["Bt"]), ("Cc", (Ccr, Cci), ["Ccr", "Cci"])]:
            for ri, src in enumerate([sr, si_]):
                CP("dve", SPW[nm][ri][0:64, :, :], src[0:64, 0:32, :], rk, ["SPW"])
                CP("pool", SPW[nm][ri][64:128, :, :], src[64:128, 32:64, :], rk, ["SPW"])
        if debug and stage == 25:
            def DUMP(name, ap, dt, rk):
                shp = [ap.shape[0], int(np.prod(ap.shape[1:]))]
                dbg[name] = nc.dram_tensor("dbg_" + name, shp, dt, kind="ExternalOutput").ap()
                P.dma("sp", dbg[name], ap, r=rk, w=["dbg" + name])
            f2 = lambda t_: t_[:].rearrange("p a b -> p (a b)") if len(t_.shape) == 3 else t_[:]
            for nm, t_, dt_, rk in [("dtt", dtt, F32, ["dtt"]), ("mag", mag, F32, ["mag"]), ("inv", inv, F32, ["inv"]), ("th", th, F32, ["th"]),
                                    ("kf", kf, F32, ["kf"]), ("rr", rr, F32, ["rr"]), ("sn", sn, F32, ["sn"]), ("cs", cs, F32, ["cs"]),
                                    ("are", are, F32, ["are"]), ("aim", aim, F32, ["aim"]), ("cfr", cfr, F32, ["cfr"]), ("cfi", cfi, F32, ["cfi"]),
                                    ("posr", posr, F32, ["pos"]), ("posi", posi, F32, ["pos"]), ("negr", negr, F32, ["neg"]), ("negi", negi, F32, ["neg"]),
                                    ("Btr", Btr, BF16, ["Bt"]), ("Bti", Bti, BF16, ["Bt"]), ("Ccr", Ccr, BF16, ["Ccr"]), ("Cci", Cci, BF16, ["Cci"]),
                                    ("PWZr", PW["PWZ"][0], BF16, ["PWZ"]), ("PWYi", PW["PWY"][1], BF16, ["PWY"]),
                                    ("AR2", AR2, F32, ["AR2"]), ("AI2", AI2, F32, ["AI2"]), ("maskf", maskf, F32, ["maskf"]), ("maskb", maskb, F32, ["maskb"]),
                                    ("drep", drep, F32, ["drep"])]:
                DUMP(nm, f2(t_), dt_, rk)
            return nc, P, es, dbg
    P.fence()
    P.es = es5

    gtp = []
    for par in range(2):
        gtp.append({nm: P.sb("%s_%d" % (nm, par), [128, 256], BF16) for nm in ["g1r", "g1i", "g2r", "g2i", "g3r", "g3i"]})
    gw = [P.sb("gw%d" % i, [128, 16, 16], F32) for i in range(8)]

    def outer(o_r, o_i, pw, Y, g, e, okeys, neg_im=False):
        shp = (128, 16, 16)
        Xr = pw[0][:, g, :][:, :, None].broadcast_to(shp)
        Xi = pw[1][:, g, :][:, :, None].broadcast_to(shp)
        yr = Y[0][:, g, :][:, None, :].broadcast_to(shp)
        yi = Y[1][:, g, :][:, None, :].broadcast_to(shp)
        o_r4 = o_r[:].rearrange("p (j c) -> p j c", c=16)
        o_i4 = o_i[:].rearrange("p (j c) -> p j c", c=16)
        o0 = 0 if e == "dve" else 4
        a, b, c_, d_ = gw[o0], gw[o0 + 1], gw[o0 + 2], gw[o0 + 3]
        ka, kb, kc, kd = ["gw%d" % (o0 + i) for i in range(4)]
        rk = ["SPW"]
        TT(e, a[:], Xr, yr, ALU.mult, rk, [ka])
        TT(e, b[:], Xi, yi, ALU.mult, rk, [kb])
        TT(e, c_[:], Xr, yi, ALU.mult, rk, [kc])
        TT(e, d_[:], Xi, yr, ALU.mult, rk, [kd])
        TT(e, o_r4, a[:], b[:], ALU.subtract, [ka, kb], okeys)
        if neg_im:
            P.op(e, lambda: veng(e).scalar_tensor_tensor(out=o_i4, in0=c_[:], scalar=-1.0, in1=d_[:], op0=ALU.mult, op1=ALU.subtract),
                 r=[kc, kd], w=okeys) if e == "dve" else (
                TS(e, c_[:], c_[:], -1.0, None, ALU.mult, None, [kc], [kc]),
                TT(e, o_i4, c_[:], d_[:], ALU.subtract, [kc, kd], okeys))
        else:
            TT(e, o_i4, c_[:], d_[:], ALU.add, [kc, kd], okeys)

    Sb = P.sb("Sb", [128, 2, 32, 258], BF16)
    Wc.update(chunks=chunks2, i=0, loaded=-1)
    Wc["wst"] = [P.sb("wst2_%d" % i, [128, 1024], F32) for i in range(3)]
    Wc["wob"] = [P.sb("wob2_%d" % i, [128, 1024], BF16) for i in range(3)]
    esz = ExitStack()
    P.es = esz
    Zs = P.sb("Zs", [128, 2, 32, 258], F32)
    WZT = [P.sb("WZT%d" % i, [128, 4, 128], BF16) for i in range(2)]
    for g in range(32):
        w_step(["act"])
        par = g % 2
        gt = gtp[par]
        e = "dve" if g % 2 == 0 else "pool"
        outer(gt["g1r"], gt["g1i"], SPW["PWZ"], SPW["Bt"], g, e, ["g1_%d" % par])
        pv = pbb[2 + par].rearrange("p (a m) -> p a m", m=128)
        for ri in range(2):
            src = gt["g1r"] if ri == 0 else gt["g1i"]
            for hf in range(2):
                TR(pv[:, ri * 2 + hf, :], src[:, 128 * hf:128 * hf + 128], ident[:], ["g1_%d" % par, "ident"], [PK[2 + par]])
        CP("act", WZT[par][:], pv[:, 0:4, :], [PK[2 + par]], ["WZT%d" % par])
        for ri in range(2):
            bk = 4 + 2 * par + ri
            for hf in range(2):
                MM(pb[bk][:, 0:257], WZT[par][:, ri * 2 + hf, :], U[:, g, hf, 0:257], hf == 0, hf == 1, ["WZT%d" % par, "U"], [PK[bk]])
            CP("dve" if ri == 0 else "act", Zs[:, ri, g, 0:257], pb[bk][:, 0:257], [PK[bk]], ["ZsF", "ZsB"])
    rt = [P.sb("rt%d" % i, [128, 2, 32], F32) for i in range(4)]
    fw, bw = slice(0, 64), slice(64, 128)
    for i in range(1, 257):
        if i % 3 == 0:
            w_step(["act"])
        n = i
        TT("dve", rt[0][fw], AR2[fw], Zs[fw, :, :, n - 1], ALU.mult, ["AR2", "ZsF"], ["rt0"])
        TT("dve", rt[1][fw], AI2[fw], Zs[fw, ::-1, :, n - 1], ALU.mult, ["AI2", "ZsF"], ["rt1"])
        TT("dve", rt[0][fw], rt[0][fw], rt[1][fw], ALU.add, ["rt0", "rt1"], ["rt0"])
        TT("dve", Zs[fw, :, :, n], Zs[fw, :, :, n], rt[0][fw], ALU.add, ["ZsF", "rt0"], ["ZsF"])
        n = 256 - i
        TT("pool", rt[2][bw], AR2[bw], Zs[bw, :, :, n + 1], ALU.mult, ["AR2", "ZsB"], ["rt2"])
        TT("pool", rt[3][bw], AI2[bw], Zs[bw, ::-1, :, n + 1], ALU.mult, ["AI2", "ZsB"], ["rt3"])
        TT("pool", rt[2][bw], rt[2][bw], rt[3][bw], ALU.add, ["rt2", "rt3"], ["rt2"])
        TT("pool", Zs[bw, :, :, n], Zs[bw, :, :, n], rt[2][bw], ALU.add, ["ZsB", "rt2"], ["ZsB"])
    MEMSET("pool", Sb[fw, :, :, 0:1], 0.0, ["Sb"])
    MEMSET("pool", Sb[bw, :, :, 256:258], 0.0, ["Sb"])
    CP("dve", Sb[fw, :, :, 1:258], Zs[fw, :, :, 0:257], ["ZsF"], ["Sb"])
    CP("act", Sb[bw, :, :, 0:256], Zs[bw, :, :, 1:257], ["ZsB"], ["Sb"])
    if debug and stage == 3:
        dbg["Zs"] = nc.dram_tensor("dbg_Zs", [128, 2 * 32 * 258], F32, kind="ExternalOutput").ap()
        P.dma("sp", dbg["Zs"], Zs[:].rearrange("p r g n -> p (r g n)"), r=["ZsF", "ZsB"], w=["dbgZ"])
        return nc, P, es, dbg
    P.fence()
    esz.close()
    P.es = es5
    YY = P.sb("YY", [128, 2, 16, 32, 16], BF16)
    Toep2 = [P.sb("Toep%d" % i, [128, 2, 256], BF16) for i in range(2)]
    diag2 = [P.sb("diag%d" % i, [128, 128], BF16) for i in range(2)]
    tq2 = [[P.sb("tq%d_%d" % (i, j), [128, 256], F32) for j in range(2)] for i in range(2)]
    ysq = P.sb("ysq", [128, 257], F32)
    yin = P.sb("yin", [128, 257], F32)
    ygh = P.sb("ygh", [128, 260], BF16)
    def p2_tables(g):
        par = g % 2
        gt = gtp[par]
        outer(gt["g1r"], gt["g1i"], SPW["PWA"], SPW["Bt"], g, "dve", ["g1_%d" % par])
        outer(gt["g2r"], gt["g2i"], SPW["PWB"], SPW["Cc"], g, "dve", ["g2_%d" % par], neg_im=True)
        outer(gt["g3r"], gt["g3i"], SPW["PWY"], SPW["Cc"], g, "pool", ["g3_%d" % par], neg_im=True)

    p2_tables(0)
    for g in range(32):
        for _ in range(2):
            w_step(["act"])
        par = g % 2
        gt = gtp[par]
        Toep, diag, tq = Toep2[par], diag2[par], tq2[par]
        kg1, kg2, kg3, kT, kD = "g1_%d" % par, "g2_%d" % par, "g3_%d" % par, "Toep%d" % par, "diag%d" % par
        if g + 1 < 32:
            p2_tables(g + 1)
        if debug and stage == 31:
            return nc, P, es, dbg
        for hf in range(2):
            for d in range(2):
                bd = 2 * par + d
                ds_ = slice(64 * d, 64 * d + 64)
                MM(pb[bd][:, 0:256], gt["g1r"][ds_, 128 * hf:128 * hf + 128], gt["g2r"][ds_, :], True, False, [kg1, kg2], [PK[bd]])
                MM(pb[bd][:, 0:256], gt["g1i"][ds_, 128 * hf:128 * hf + 128], gt["g2i"][ds_, :], False, True, [kg1, kg2], [PK[bd]])
            TT("dve", tq[0][:], maskf[:, hf, :], pb[2 * par][:, 0:256], ALU.mult, ["maskf", PK[2 * par]], ["tq0_%d" % par])
            TT("dve", tq[1][:], maskb[:, hf, :], pb[2 * par + 1][:, 0:256], ALU.mult, ["maskb", PK[2 * par + 1]], ["tq1_%d" % par])
            TT("pool", Toep[:, hf, :], tq[0][:], tq[1][:], ALU.add, ["tq0_%d" % par, "tq1_%d" % par], [kT])
        if debug and stage == 32:
            return nc, P, es, dbg
        ACT(diag[:], ident[:], AF.Copy, ["ident", "drep"], [kD], scale=drep[:, g:g + 1])
        if debug and stage == 33:
            return nc, P, es, dbg
        for ho in range(2):
            bk = 4 + ho
            cs_ = slice(128 * ho, 128 * ho + 128)
            MM(pb[bk][:, 0:257], Toep[:, 0, cs_], U[:, g, 0, 0:257], True, False, [kT, "U"], [PK[bk]])
            MM(pb[bk][:, 0:257], Toep[:, 1, cs_], U[:, g, 1, 0:257], False, False, [kT, "U"], [PK[bk]])
            MM(pb[bk][:, 0:257], diag[:], U[:, g, ho, 0:257], False, False, [kD, "U"], [PK[bk]])
            MM(pb[bk][:, 0:257], gt["g3r"][:, cs_], Sb[:, 0, g, 0:257], False, False, [kg3, "Sb"], [PK[bk]])
            MM(pb[bk][:, 0:257], gt["g3i"][:, cs_], Sb[:, 1, g, 0:257], False, True, [kg3, "Sb"], [PK[bk]])
            if debug and stage == 34:
                return nc, P, es, dbg
            yp = pb[bk][:, 0:257]
            ACT(ysq[:], yp, AF.Square, [PK[bk]], ["ysq"])
            TS("dve", ysq[:], ysq[:], 0.044715, 1.0, ALU.mult, ALU.add, ["ysq"], ["ysq"])
            TT("dve", yin[:], ysq[:], yp, ALU.mult, ["ysq", PK[bk]], ["yin"])
            ACT(yin[:], yin[:], AF.Sigmoid, ["yin"], ["yin"], scale=1.5957691216057308)
            TT("dve", ygh[:, 1:258], yin[:], yp, ALU.mult, ["yin", PK[bk]], ["ygh"])
            if debug and stage == 35:
                return nc, P, es, dbg
            for sbi in range(2):
                pbi = 6 + sbi
                TR(pbb[pbi][:, 0:128], ygh[:, 2 + 128 * sbi:130 + 128 * sbi], ident[:], ["ygh", "ident"], [PK[pbi]])
                CP("act", YY[:, sbi, 8 * ho:8 * ho + 8, g, :], pbb[pbi][:, 0:128].rearrange("p (t c) -> p t c", c=16),
                   [PK[pbi]], ["YY"])
    while w_step(["act", "dve"]):
        pass
    if debug and stage == 36:
        return nc, P, es, dbg
    for sbi in range(2):
        r0 = 16 + 2048 * sbi
        P.dma("sp", ygd[r0:r0 + 2048, :].rearrange("(n t) c -> n (t c)", t=16),
              YY[:, sbi].rearrange("p t g c -> p (t g c)"), r=["YY"], w=["ygd"])
    P.fence()
    es5.close()
    P.es = es
    if debug and stage == 4:
        dbg["yg"] = nc.dram_tensor("dbg_yg", [L, 512], BF16, kind="ExternalOutput").ap()
        P.dma("sp", dbg["yg"][16:L, :], ygd[16:L, :], r=["ygd"], w=["dbgyg"])
        return nc, P, es, dbg
    if debug and stage == 5:
        dbg["Gt"] = nc.dram_tensor("dbg_Gt", [128, 8 * GW], BF16, kind="ExternalOutput").ap()
        P.dma("sp", dbg["Gt"], Gt[:].rearrange("p h m -> p (h m)"), r=["Gt"], w=["dbgGt"])
        return nc, P, es, dbg

    eskv = ExitStack()
    P.es = eskv
    KT = P.sb("KT", [128, 8, L], BF16)
    Vp = P.sb("Vp", [128, 33, 8, 129], BF16)
    MEMSET("pool", Vp[:, :, :, 128:129], 1.0, ["Vp"])
    OFF_Q, OFF_K, OFF_V, OFF_GS, OFF_GA = 512, 1536, 2560, 3584, 4608

    def wchunk(dst, src2d, nk, r, w):
        P.dma("sp", dst, src2d.rearrange("(k p) n -> p k n", p=128), r=r, w=w)

    def qk_proj(dst, wch, wkey, hnT_, hkey, ntok, gain, sqq, rstd, dkey, par=0, split=None,
                sqkeys=("sqq0", "sqq1", "sqq2", "sqq3"), stage=0):
        pa, pq = par % 4, 4 + par % 4
        sq_, rs_ = sqq[par % 4], rstd[par % 2]
        ks, kr = sqkeys[par % 4], "rstdq%d" % (par % 2)
        if stage in (0, 1):
            for k in range(8):
                MM(pb[pa][:, 0:ntok], wch[:, k, :], hnT_[:, k, 0:ntok], k == 0, k == 7, [wkey, hkey], [PK[pa]])
            ACT(sq_[:, 0:ntok], pb[pa][:, 0:ntok], AF.Square, [PK[pa]], [ks])
        if stage in (0, 2):
            MM(pb[pq][:, 0:ntok], bones[:], sq_[:, 0:ntok], True, True, ["bones", ks], [PK[pq]])
            ACT(rs_[:, 0:ntok], pb[pq][:, 0:ntok], AF.Ln, [PK[pq], "epsb"], [kr], bias=epsb[:, 0:1])
            ACT(rs_[:, 0:ntok], rs_[:, 0:ntok], AF.Exp, [kr], [kr], scale=-0.5)
            if split is None:
                STT(dst, pb[pa][:, 0:ntok], gain[:, 0:1], rs_[:, 0:ntok], ALU.mult, ALU.mult, [PK[pa], kr, "qg", "kg"], [dkey])
            else:
                for hf_, d_ in enumerate(split):
                    sl_ = slice(64 * hf_, 64 * hf_ + 64)
                    STT(d_, pb[pa][sl_, 0:ntok], gain[sl_, 0:1], rs_[sl_, 0:ntok], ALU.mult, ALU.mult,
                        [PK[pa], kr, "qg", "kg", "QTz"], [dkey])

    with ExitStack() as e1b:
        P.es = e1b
        hnTs = [P.sb("hnT1_%d" % i, [128, 8, 512], BF16) for i in range(2)]
        wk = [P.sb("wk%d" % i, [128, 8, 128], BF16) for i in range(4)]
        wv = [P.sb("wv%d" % i, [128, 8, 512], BF16) for i in range(2)]
        sqq = [P.sb("sqq1_%d" % i, [128, 512], BF16) for i in range(4)]
        rstdq = [P.sb("rstdq1_%d" % i, [128, 512], F32) for i in range(2)]
        wi = 0
        for s_ in range(9):
            hnT = hnTs[s_ % 2]
            hk1 = "hnT1_%d" % (s_ % 2)
            ntok, pos0 = (512, 16 + 512 * s_) if s_ < 8 else (16, 0)
            P.dma("sp", hnT[:, :, 0:ntok], hnTd[:, :, pos0:pos0 + ntok], r=["hnTd0", "hnTd1", "hnTd2"], w=[hk1])
            def kp(h, st):
                qk_proj(KT[:, h, pos0:pos0 + ntok], wk[h % 4], "wk%d" % (h % 4), hnT, hk1, ntok, kg, sqq, rstdq, "KT",
                        par=h, stage=st)

            for h in range(10):
                if h < 8:
                    wchunk(wk[h % 4][:], wb_in[:, OFF_K + 128 * h:OFF_K + 128 * h + 128], 8, ["wb_in"], ["wk%d" % (h % 4)])
                    kp(h, 1)
                if h >= 2:
                    kp(h - 2, 2)
            for half in range(2):
                wchunk(wv[half][:], wb_in[:, OFF_V + 512 * half:OFF_V + 512 * half + 512], 8, ["wb_in"], ["wv%d" % half])
                ntile = 4 if s_ < 8 else 1
                for j in range(ntile):
                    npt = 128 if s_ < 8 else 16
                    kt = 1 + 4 * s_ + j if s_ < 8 else 0
                    bk = (half * 4 + j) % 4
                    for k in range(8):
                        MM(pb[bk][0:npt, :], hnT[:, k, 128 * j:128 * j + npt], wv[half][:, k, :], k == 0, k == 7,
                           [hk1, "wv%d" % half], [PK[bk]])
                    CP("act" if j % 2 else "dve", Vp[0:npt, kt, 4 * half:4 * half + 4, 0:128],
                       pb[bk][0:npt, :].rearrange("p (h e) -> p h e", e=128), [PK[bk]], ["Vp"])
    P.fence()
    P.es = eskv
    if debug and stage == 6:
        dbg["KT"] = nc.dram_tensor("dbg_KT", [128, 8 * L], BF16, kind="ExternalOutput").ap()
        P.dma("sp", dbg["KT"], KT[:].rearrange("p h m -> p (h m)"), r=["KT"], w=["dbgKT"])
        dbg["Vp"] = nc.dram_tensor("dbg_Vp", [128, 33 * 8 * 129], BF16, kind="ExternalOutput").ap()
        P.dma("sp", dbg["Vp"], Vp[:].rearrange("p a h m -> p (a h m)"), r=["Vp"], w=["dbgVp"])
        return nc, P, es, dbg

    nsb = 8 if stage >= 8 else 1
    for s_ in range(nsb):
        pos0 = 16 + 512 * s_
        esA = ExitStack()
        P.es = esA
        hnT = P.sb("hnT3", [128, 8, 512], BF16)
        onT = P.sb("onT", [128, 8, 512], BF16)
        P.dma("sp", hnT[:], hnTd[:, :, pos0:pos0 + 512], r=["hnTd0", "hnTd1", "hnTd2"], w=["hnT3"])
        with ExitStack() as eat:
            P.es = eat
            QT = P.sb("QT", [128, 8, 2, 512], BF16)
            MEMSET("pool", QT[:].rearrange("p h t q -> p (h t q)"), 0.0, ["QTz"])
            wq = [P.sb("wq%d" % i, [128, 8, 128], BF16) for i in range(2)]
            rstdq = [P.sb("rstdq3_%d" % i, [128, 512], F32) for i in range(2)]
            pt = [P.sb("pt%d" % i, [128, 512], BF16) for i in range(4)]
            sqq = pt[0:4]
            Osb = [P.sb("Osb0", [128, 4, 258], F32)] * 2
            dn = P.sb("dn", [128, 2], F32)
            otmp = P.sb("otmp", [128, 128], F32)
            of = P.sb("of", [128, 128], F32)
            osq = P.sb("osq", [128, 128], F32)
            oss = P.sb("oss", [128, 1], F32)
            onb = P.sb("onb", [128, 2, 4, 128], BF16)
            def qp(h, st):
                b = h % 2
                qk_proj(None, wq[b], "wq%d" % b, hnT, "hnT3", 512, qg, sqq, rstdq, "QT%d" % h, par=h,
                        split=(QT[0:64, h, 0, :], QT[64:128, h, 1, :]), sqkeys=("pt0", "pt1", "pt2", "pt3"), stage=st)

            for h in range(8):
                wchunk(wq[h % 2][:], wb_in[:, OFF_Q + 128 * h:OFF_Q + 128 * h + 128], 8, ["wb_in"], ["wq%d" % (h % 2)])
                qp(h, 0)
            units = [(h, t, kt) for h in range(8) for t in range(2) for kt in range(33)]

            def geom(kt):
                return (16, 0) if kt == 0 else (128, 16 + 128 * (kt - 1))

            def emit_S(i):
                h, t, kt = units[i]
                nk, kp0 = geom(kt)
                ts_ = slice(64 * t, 64 * t + 64)
                cls = []
                for qt in range(4):
                    delta = kp0 - (pos0 + 128 * qt)
                    cls.append("pos" if delta >= 218 else ("neg" if delta <= -90 - nk else "near"))
                sbk = (0, 1, 6)[i % 3]
                pti = i % 4
                S = pb[sbk]
                mixed = len(set(cls)) > 1 or cls[0] == "near"
                MM(S[0:nk, :], KT[:, h, kp0:kp0 + nk], QT[:, h, t, :], True, not mixed,
                   ["KT", "QT%d" % h, "QTz"], [PK[sbk]])
                if mixed:
                    for qt in range(4):
                        if cls[qt] == "near":
                            off = (pos0 + 128 * qt) - kp0 + GD
                        elif cls[qt] == "pos":
                            off = 0
                        else:
                            off = 570
                        MM(S[0:nk, 128 * qt:128 * qt + 128], ident[0:nk, 0:nk], Gt[0:nk, h, off:off + 128], False,
                           qt == 3, ["ident", "Gt"], [PK[sbk]])
                    ACT(pt[pti][0:nk, :], S[0:nk, :], AF.Exp, [PK[sbk]], ["pt%d" % pti], scale=0.125)
                elif cls[0] == "pos":
                    ACT(pt[pti][0:nk, :], S[0:nk, :], AF.Exp, [PK[sbk], "farb"], ["pt%d" % pti], scale=0.125,
                        bias=farb[0:nk, 1, h:h + 1])
                else:
                    ACT(pt[pti][0:nk, :], S[0:nk, :], AF.Exp, [PK[sbk]], ["pt%d" % pti], scale=0.125)

            def emit_PV(i):
                h, t, kt = units[i]
                nk, kp0 = geom(kt)
                pti = i % 4
                for qt in range(4):
                    ob = 2 + 2 * t + qt // 2
                    c0 = 129 * (qt % 2)
                    MM(pb[ob][:, c0:c0 + 129], pt[pti][0:nk, 128 * qt:128 * qt + 128], Vp[0:nk, kt, h, :],
                       kt == 0 and qt % 2 == 0, kt == 32, ["pt%d" % pti, "Vp"], [PK[ob]], skip_group_check=True)

            def epilogue(h):
                ob_ = Osb[h % 2]
                ok = "Osb0"
                for j in range(4):
                    CP("dve", ob_[:, j, :], pb[2 + j][:, 0:258], [PK[2 + j]], [ok])
                for qt in range(4):
                    c0 = 129 * (qt % 2)
                    O0 = ob_[:, qt // 2, :]
                    O1 = ob_[:, 2 + qt // 2, :]
                    CP("dve", dn[:, 0:1], O0[:, c0 + 128:c0 + 129], [ok], ["dn"])
                    CP("dve", dn[:, 1:2], O1[:, c0 + 128:c0 + 129], [ok], ["dn"])
                    RECIP(dn[:], dn[:], ["dn"], ["dn"])
                    TT("dve", dn[:, 1:2], dn[:, 1:2], lamt[:], ALU.mult, ["dn", "lamt"], ["dn"])
                    TS("dve", otmp[:], O1[:, c0:c0 + 128], dn[:, 1:2], None, ALU.mult, None, [ok, "dn"], ["otmp"])
                    STT(of[:], O0[:, c0:c0 + 128], dn[:, 0:1], otmp[:], ALU.mult, ALU.subtract, [ok, "dn", "otmp"], ["of"])
                    TT("dve", osq[:], of[:], of[:], ALU.mult, ["of"], ["osq"])
                    P.op("dve", lambda: nc.vector.reduce_sum(out=oss[:], in_=osq[:], axis=AX.X), r=["osq"], w=["oss"])
                    TS("dve", oss[:], oss[:], 1.0 / 128, EPS, ALU.mult, ALU.add, ["oss"], ["oss"])
                    TT("pool", oss[:], oss[:], mhalf[:], ALU.pow, ["oss", "mhalf"], ["oss"])
                    TS("dve", onb[:, h % 2, qt, :], of[:], oss[:, 0:1], None, ALU.mult, None, ["of", "oss"], ["onb%d" % (h % 2)])

            def tr_heads(h):
                pv = pbb[7].rearrange("p (q t) -> p q t", t=128)
                for qt in range(4):
                    TR(pv[:, qt, :], onb[:, h % 2, qt, :], ident[:], ["onb%d" % (h % 2), "ident"], [PK[7]])
                CP("dve", onT[:, h, :].rearrange("p (q t) -> p q t", t=128), pv[:, 0:4, :], [PK[7]], ["onT"])

            n_u = len(units)
            for i in range(n_u + 2):
                if i < n_u:
                    emit_S(i)
                if i >= 2:
                    emit_PV(i - 2)
                    h_, t_, kt_ = units[i - 2]
                    if t_ == 1 and kt_ == 32:
                        epilogue(h_)
                        if h_ >= 1:
                            tr_heads(h_ - 1)
            tr_heads(7)
        P.fence()
        P.es = esA
        if debug and stage == 7:
            dbg["onT"] = nc.dram_tensor("dbg_onT", [128, 8 * 512], BF16, kind="ExternalOutput").ap()
            P.dma("sp", dbg["onT"], onT[:].rearrange("p h m -> p (h m)"), r=["onT"], w=["dbgonT"])
            return nc, P, es, dbg
        emm = ExitStack()
        P.es = emm
        mT = P.sb("mT", [128, 8, 512], BF16)
        with ExitStack() as emg:
            P.es = emg
            ygT = P.sb("ygT", [128, 4, 512], BF16)
            ygk = [P.sb("ygk%d" % i, [128, 512], BF16) for i in range(2)]
            ring8 = [P.sb("r8_%d" % i, [128, 8, 128], BF16) for i in range(3)]
            ring4 = [P.sb("r4_%d" % i, [128, 4, 128], BF16) for i in range(2)]
            gas = P.sb("gas", [128, 512], BF16)
            gss = P.sb("gss", [128, 512], BF16)
            sbs = P.sb("sbs", [128, 512], BF16)
            tg = P.sb("tg", [128, 512], BF16)
            m1 = P.sb("m1", [128, 512], BF16)
            m2 = P.sb("m2", [128, 512], BF16)
            r8i = 0
            r4i = 0
            for f in range(8):
                fs = slice(128 * f, 128 * f + 128)
                w8 = []
                for src in [wb_ao[:, fs], wb_in[:, OFF_GA + 128 * f:OFF_GA + 128 * f + 128], wb_in[:, OFF_GS + 128 * f:OFF_GS + 128 * f + 128]]:
                    i_ = r8i % 3
                    r8i += 1
                    wchunk(ring8[i_][:], src, 8, ["wb_ao", "wb_in"], ["r8_%d" % i_])
                    w8.append((ring8[i_], "r8_%d" % i_))
                w4 = []
                for src in [wb_a[:, fs], wb_b[:, fs]]:
                    i_ = r4i % 2
                    r4i += 1
                    wchunk(ring4[i_][:], src, 4, ["wb_a", "wb_b"], ["r4_%d" % i_])
                    w4.append((ring4[i_], "r4_%d" % i_))
                for k in range(8):
                    MM(pb[0][:, :], w8[0][0][:, k, :], onT[:, k, :], k == 0, k == 7, [w8[0][1], "onT"], [PK[0]])
                for k in range(8):
                    MM(pb[1][:, :], w8[1][0][:, k, :], hnT[:, k, :], k == 0, k == 7, [w8[1][1], "hnT3"], [PK[1]])
                for k in range(8):
                    MM(pb[4][:, :], w8[2][0][:, k, :], hnT[:, k, :], k == 0, k == 7, [w8[2][1], "hnT3"], [PK[4]])
                if f == 0:
                    for j in range(4):
                        b = j % 2
                        P.dma("sp", ygk[b][:], ygd[pos0 + 128 * j:pos0 + 128 * j + 128, :], r=["ygd"], w=["ygk%d" % b])
                        pv = pbb[6 + b].rearrange("p (c t) -> p c t", t=128)
                        for c in range(4):
                            TR(pv[:, c, :], ygk[b][:, 128 * c:128 * c + 128], ident[:], ["ygk%d" % b, "ident"], [PK[6 + b]])
                        CP("dve", ygT[:, :, 128 * j:128 * j + 128], pv[:, 0:4, :], [PK[6 + b]], ["ygT"])
                for k in range(4):
                    MM(pb[2][:, :], w4[0][0][:, k, :], ygT[:, k, :], k == 0, k == 3, [w4[0][1], "ygT"], [PK[2]])
                for k in range(4):
                    MM(pb[3][:, :], w4[1][0][:, k, :], ygT[:, k, :], k == 0, k == 3, [w4[1][1], "ygT"], [PK[3]])
                ACT(gas[:], pb[1][:, :], AF.Sigmoid, [PK[1]], ["gas"])
                ACT(gss[:], pb[4][:, :], AF.Sigmoid, [PK[4]], ["gss"])
                ACT(sbs[:], pb[3][:, :], AF.Sigmoid, [PK[3]], ["sbs"])
                TT("dve", m1[:], gas[:], pb[0][:, :], ALU.mult, ["gas", PK[0]], ["m1"])
                TT("pool", tg[:], sbs[:], gss[:], ALU.mult, ["sbs", "gss"], ["tg"])
                TT("dve", m2[:], tg[:], pb[2][:, :], ALU.mult, ["tg", PK[2]], ["m2"])
                TT("pool", mT[:, f, :], m1[:], m2[:], ALU.add, ["m1", "m2"], ["mT"])
        P.fence()
        P.es = emm
        with ExitStack() as emo:
            P.es = emo
            wo = [P.sb("wo%d" % i, [128, 8, 512], BF16) for i in range(2)]
            xh = [P.sb("xh%d" % i, [128, 512], F32) for i in range(3)]
            xi_ = 0
            for c in range(2):
                wchunk(wo[c][:], wb_o[:, 512 * c:512 * c + 512], 8, ["wb_o"], ["wo%d" % c])
                for j in range(4):
                    r0 = 512 * s_ + 128 * j
                    xb = xi_ % 3
                    xi_ += 1
                    P.dma("sp", xh[xb][:], x[r0:r0 + 128, 512 * c:512 * c + 512], w=["xh%d" % xb])
                    bk = 2 + j
                    for k in range(8):
                        MM(pb[bk][:, :], mT[:, k, 128 * j:128 * j + 128], wo[c][:, k, :], k == 0, k == 7,
                           ["mT", "wo%d" % c], [PK[bk]])
                    TT("dve", xh[xb][:], xh[xb][:], pb[bk][:, :], ALU.add, ["xh%d" % xb, PK[bk]], ["xh%d" % xb])
                    P.dma("sp", out[r0:r0 + 128, 512 * c:512 * c + 512], xh[xb][:], r=["xh%d" % xb], w=["outd"])
        P.fence()
        emm.close()
        esA.close()
    P.fence()
    eskv.close()
    P.es = es
    if debug and stage in (8, 1008):
        return nc, P, es, dbg

    with ExitStack() as eff:
        P.es = eff
        h2t = [P.sb("h2t%d" % i, [128, 4, D], F32) for i in range(2)]
        hn2T = [P.sb("hn2T%d" % i, [128, 8, 512], BF16) for i in range(2)]
        xn4 = [P.sb("xn4_%d" % i, [128, D], BF16) for i in range(4)]
        ss4 = [P.sb("ss4_%d" % i, [128, 1], F32) for i in range(4)]
        fT = P.sb("fT", [128, 32, 512], BF16)
        w1 = [P.sb("w1_%d" % i, [128, 8, 512], BF16) for i in range(2)]
        w2 = [P.sb("w2_%d" % i, [128, 8, 512], BF16) for i in range(2)]
        rl = [P.sb("rl%d" % i, [128, 512], BF16) for i in range(2)]
        w2i = 0

        def prep_a(s_):
            p = s_ % 2
            for j in range(4):
                r0 = 512 * s_ + 128 * j
                hk_ = "h2t%d_%d" % (p, j)
                P.dma("sp", h2t[p][:, j, :], out[r0:r0 + 128, :], w=[hk_])
                ACT(xn4[j][:], h2t[p][:, j, :], AF.Square, [hk_], ["xn4_%d" % j, "ss4_%d" % j], accum_out=ss4[j][:])
                TS("dve", ss4[j][:], ss4[j][:], 1.0 / D, EPS, ALU.mult, ALU.add, ["ss4_%d" % j], ["ss4_%d" % j])
                TT("pool", ss4[j][:], ss4[j][:], mhalf[:], ALU.pow, ["ss4_%d" % j, "mhalf"], ["ss4_%d" % j])
                TS("dve", xn4[j][:], h2t[p][:, j, :], ss4[j][:, 0:1], None, ALU.mult, None, [hk_, "ss4_%d" % j], ["xn4_%d" % j])

        def prep_b(s_):
            p = s_ % 2
            for j in range(4):
                pbi = 6 + j % 2
                pv = pbb[pbi].rearrange("p (k t) -> p k t", t=128)
                for k in range(8):
                    TR(pv[:, k, :], xn4[j][:, k * 128:(k + 1) * 128], ident[:], ["xn4_%d" % j, "ident"], [PK[pbi]])
                CP("dve", hn2T[p][:, :, 128 * j:128 * j + 128], pv[:, :, :], [PK[pbi]], ["hn2T%d" % p])

        prep_a(0)
        prep_b(0)
        for s_ in range(8):
            p = s_ % 2
            if s_ + 1 < 8:
                prep_a(s_ + 1)
            for ft in range(32):
                b3 = (ft // 4) % 2
                if ft % 4 == 0:
                    wchunk(w1[b3][:], wb_f1[:, 128 * ft:128 * ft + 512], 8, ["wb_f1"], ["w1_%d" % b3])
                bk = ft % 2
                fo = 128 * (ft % 4)
                for k in range(8):
                    MM(pb[bk][:, :], w1[b3][:, k, fo:fo + 128], hn2T[p][:, k, :], k == 0, k == 7, ["w1_%d" % b3, "hn2T%d" % p], [PK[bk]])
                ACT(rl[bk][:], pb[bk][:, :], AF.Relu, [PK[bk]], ["rl%d" % bk])
                TT("pool", fT[:, ft, :], rl[bk][:], rl[bk][:], ALU.mult, ["rl%d" % bk], ["fT"])
            if s_ + 1 < 8:
                prep_b(s_ + 1)
            for half in range(2):
                for kg in range(4):
                    wb_ = w2i % 2
                    w2i += 1
                    wchunk(w2[wb_][:], wb_f2[1024 * kg:1024 * kg + 1024, 512 * half:512 * half + 512], 8, ["wb_f2"], ["w2_%d" % wb_])
                    for j in range(4):
                        for k8 in range(8):
                            MM(pb[2 + j][:, :], fT[:, 8 * kg + k8, 128 * j:128 * j + 128], w2[wb_][:, k8, :],
                               kg == 0 and k8 == 0, kg == 3 and k8 == 7, ["fT", "w2_%d" % wb_], [PK[2 + j]])
                for j in range(4):
                    hk_ = "h2t%d_%d" % (p, j)
                    TT("dve", h2t[p][:, j, 512 * half:512 * half + 512], h2t[p][:, j, 512 * half:512 * half + 512], pb[2 + j][:, :],
                       ALU.add, [hk_, PK[2 + j]], [hk_])
            for j in range(4):
                r0 = 512 * s_ + 128 * j
                P.dma("sp", out[r0:r0 + 128, :], h2t[p][:, j, :], r=["h2t%d_%d" % (p, j)], w=["outd"])
    P.fence()
    P.es = es
    return nc, P, es, dbg


def _in_maps(inputs):
    f = lambda a: np.ascontiguousarray(np.asarray(a, dtype=np.float32))
    oh = _onehot_const()
    common = {
        "meta": f(inputs["meta_tokens"]), "relb": f(inputs["rel_bias_table"]), "oh": oh,
        "nmix": f(inputs["norm_mix"]).reshape(1, D), "w_in": f(inputs["w_in"])[0],
        "lamre": f(inputs["s5_lambda_re"])[0], "lamim": f(inputs["s5_lambda_im"])[0],
        "lstep": f(inputs["s5_log_step"]).reshape(1, 64),
        "bre": f(inputs["s5_b_re"])[0], "bim": f(inputs["s5_b_im"])[0],
        "cre": f(inputs["s5_c_re"]).reshape(1024, 64), "cim": f(inputs["s5_c_im"]).reshape(1024, 64),
        "s5d": f(inputs["s5_d"]).reshape(1, 512), "w_a": f(inputs["w_glu_a"])[0], "w_b": f(inputs["w_glu_b"])[0],
        "qn": f(inputs["q_norm"]).reshape(1, 64), "kn": f(inputs["k_norm"]).reshape(1, 64),
        "lq1": f(inputs["lambda_q1"]).reshape(1, 64), "lk1": f(inputs["lambda_k1"]).reshape(1, 64),
        "lq2": f(inputs["lambda_q2"]).reshape(1, 64), "lk2": f(inputs["lambda_k2"]).reshape(1, 64),
        "subln": f(inputs["attn_subln"]).reshape(1, 128), "w_ao": f(inputs["w_attn_out"])[0],
        "w_o": f(inputs["w_o"])[0], "nff": f(inputs["norm_ff"]).reshape(1, D),
        "w_f1": f(inputs["w_ff1"])[0], "w_f2": f(inputs["w_ff2"])[0],
    }
    xs = f(inputs["x"])
    return [dict(common, x=xs[b]) for b in range(xs.shape[0])]


def _run(inputs, stage=99, debug=False, cores=None, out_keys=("outd",)):
    nc, P, es, dbg = build_program(stage=stage, debug=debug)
    keys = list(out_keys) if not debug else [k for k in P.last_w if k.startswith("dbg") or k == "outd"]
    with nc.allow_non_contiguous_dma(reason="small strided parameter loads"):
        st = P.finish(keys)
    maps = _in_maps(inputs)
    if cores is not None:
        maps = maps[:cores]
    res = run_bass_kernel_spmd(nc, maps, core_ids=list(range(len(maps))))
    if not debug:
        es.close()
    return res, st


def kernel(**inputs):
    res, _ = _run(inputs)
    return np.stack([np.asarray(r["out"], dtype=np.float32) for r in res.results], axis=0)
```

```python
import numpy as np
import concourse.bass as bass
import concourse.mybir as mybir
from concourse.bass_utils import run_bass_kernel_spmd
from contextlib import ExitStack

F32 = mybir.dt.float32
BF16 = mybir.dt.bfloat16
I32 = mybir.dt.int32
AF = mybir.ActivationFunctionType
ALU = mybir.AluOpType
AX = mybir.AxisListType


class Prog:
    ENG = {"pe": "tensor", "act": "scalar", "dve": "vector", "pool": "gpsimd", "sp": "sync"}

    def __init__(self, nc, es, n_slots=12):
        self.nc = nc
        self.es = es
        self.ops = []
        self.last_w = {}
        self.readers = {}
        self.n_slots = n_slots
        self.fence_id = None

    def sb(self, name, shape, dtype):
        self.n_tensors = getattr(self, "n_tensors", 0) + 1
        return self.es.enter_context(self.nc.sbuf_tensor("%s_%d" % (name, self.n_tensors), list(shape), dtype))

    def ps(self, name, shape, dtype=F32):
        return self.es.enter_context(self.nc.psum_tensor(name, list(shape), dtype))

    def _add(self, eng, fn, r, w, dma):
        oid = len(self.ops)
        deps = {}
        for k in r:
            if k in self.last_w:
                deps[self.last_w[k]] = "hard"
        for k in w:
            if k in self.last_w:
                deps[self.last_w[k]] = "hard"
            for rd in self.readers.get(k, ()):
                if rd not in deps:
                    deps[rd] = "war"
        for k in r:
            self.readers.setdefault(k, []).append(oid)
        for k in w:
            self.last_w[k] = oid
            self.readers[k] = []
        if self.fence_id is not None and self.fence_id not in deps:
            deps[self.fence_id] = "hard"
        self.ops.append(dict(eng=eng, fn=fn, deps=deps, dma=dma))
        return oid

    def fence(self):
        nc = self.nc
        keys = list(set(self.last_w.keys()) | set(self.readers.keys()))
        self.fence_id = None
        fid = self._add("sp", lambda: nc.sync.nop(), [], keys, False)
        self.fence_id = fid
        return fid

    def op(self, eng, fn, r=(), w=()):
        return self._add(eng, fn, list(r), list(w), False)

    def dma(self, q, out, in_, r=(), w=()):
        nc = self.nc
        e = getattr(nc, self.ENG[q])
        return self._add(q, lambda: e.dma_start(out=out, in_=in_), list(r), list(w), True)

    def make_identity(self, ident, key, ones=None):
        nc = self.nc
        self.op("pool", lambda: nc.gpsimd.memset(ident[:], 1.0), w=[key])
        n = ident.shape[1]
        self.op("pool", lambda: nc.gpsimd.affine_select(
            out=ident[:], in_=ident[:], pattern=[[1, n]], compare_op=ALU.is_equal,
            fill=0.0, base=0, channel_multiplier=-1), r=[key], w=[key])

    def finish(self, out_keys):
        nc = self.nc
        self._add("sp", None, list(out_keys), [], False)
        ops = self.ops
        for o in ops:
            nd = {}
            for d, typ in o["deps"].items():
                p = ops[d]
                if not p["dma"] and not o["dma"] and p["eng"] == o["eng"]:
                    if o["eng"] == "pe":
                        continue
                nd[d] = typ
            o["deps"] = nd
        signalling = set()
        for o in ops:
            for d in o["deps"]:
                signalling.add(d)
        engs = ["pe", "act", "dve", "pool", "sp"]
        esem = {e: self.es.enter_context(nc.semaphore("s_" + e)) for e in engs}
        dsem = {}
        for q in ["sp", "pool", "act"]:
            if any(o["dma"] and o["eng"] == q for o in ops):
                dsem[q] = [self.es.enter_context(nc.semaphore("d_%s%d" % (q, i))) for i in range(self.n_slots)]
        ecount = {e: 0 for e in engs}
        dcount = {q: 0 for q in dsem}
        slot_val = {}
        slot_last = {}
        for i, o in enumerate(ops):
            if o["dma"]:
                q = o["eng"]
                s = dcount[q] % self.n_slots
                dcount[q] += 1
                v = slot_val.get((q, s), 0) + 16
                slot_val[(q, s)] = v
                o["sig"] = (("d", q, s), dsem[q][s], v)
                o["slot_prev"] = slot_last.get((q, s))
                slot_last[(q, s)] = i
            elif i in signalling:
                e = o["eng"]
                ecount[e] += 1
                o["sig"] = (("e", e), esem[e], ecount[e])
            else:
                o["sig"] = None
        seen = {}
        n_wait = 0
        for i, o in enumerate(ops):
            e = o["eng"]
            eobj = getattr(nc, self.ENG[e])
            deps = list(o["deps"].keys())
            if o["dma"] and o["slot_prev"] is not None:
                deps.append(o["slot_prev"])
            need = {}
            for d in deps:
                key, sem, val = ops[d]["sig"]
                if seen.get((e, key), 0) < val:
                    if key not in need or need[key][1] < val:
                        need[key] = (sem, val)
            for key, (sem, val) in need.items():
                eobj.wait_ge(sem, val)
                seen[(e, key)] = val
                n_wait += 1
            if o["fn"] is not None:
                ins = o["fn"]()
                if o["sig"] is not None:
                    ins.then_inc(o["sig"][1], 16 if o["dma"] else 1)
        self.stats = dict(n_ops=len(ops), n_wait=n_wait, ecount=ecount, dcount=dcount)
        return self.stats


NX = 4096
L = 4112
D = 1024
EPS = 1e-6
JW = 831
GW = 704
GD = 352
LAM_INIT = 0.2


def _rel_bucket_host(rel):
    nb = 16
    ret = (rel > 0).astype(np.int64) * nb
    n = np.abs(rel)
    nf = np.maximum(n, 1).astype(np.float32)
    large = 8 + (np.log(nf / np.float32(8)) / np.float32(np.log(16.0)) * np.float32(8)).astype(np.int64)
    large = np.minimum(large, nb - 1)
    return ret + np.where(n < 8, n, large)


def _onehot_const():
    i = np.arange(JW)
    rel = GD + 127 - i
    b = _rel_bucket_host(rel)
    oh = np.zeros((32, JW), np.float32)
    oh[b, i] = 1.0
    return oh


def sl(start, n, step=1):
    return slice(start, start + (n - 1) * step + 1, step)


def build_program(stage=99, debug=False):
    nc = bass.Bass("TRN2", target_bir_lowering=False)

    def din(name, shape):
        return nc.dram_tensor(name, list(shape), F32, kind="ExternalInput").ap()

    x = din("x", [NX, D])
    meta = din("meta", [16, D])
    relb = din("relb", [32, 8])
    oh = din("oh", [32, JW])
    nmix = din("nmix", [1, D])
    w_in = din("w_in", [D, 5632])
    lamre_d = din("lamre", [2, 32, 64])
    lamim_d = din("lamim", [2, 32, 64])
    lstep_d = din("lstep", [1, 64])
    bre_d = din("bre", [2, 32, 64, 16])
    bim_d = din("bim", [2, 32, 64, 16])
    cre_d = din("cre", [1024, 64])
    cim_d = din("cim", [1024, 64])
    s5d_d = din("s5d", [1, 512])
    w_a = din("w_a", [512, D])
    w_b = din("w_b", [512, D])
    qn_d = din("qn", [1, 64])
    kn_d = din("kn", [1, 64])
    lq1 = din("lq1", [1, 64])
    lk1 = din("lk1", [1, 64])
    lq2 = din("lq2", [1, 64])
    lk2 = din("lk2", [1, 64])
    subln_d = din("subln", [1, 128])
    w_ao = din("w_ao", [D, D])
    w_o = din("w_o", [D, D])
    nff = din("nff", [1, D])
    w_f1 = din("w_f1", [D, 4096])
    w_f2 = din("w_f2", [4096, D])
    out = nc.dram_tensor("out", [NX, D], F32, kind="ExternalOutput").ap()

    def dscr(name, shape, dt):
        return nc.dram_tensor(name, list(shape), dt).ap()

    wb_in = dscr("wb_in", [D, 5632], BF16)
    wb_a = dscr("wb_a", [512, D], BF16)
    wb_b = dscr("wb_b", [512, D], BF16)
    wb_ao = dscr("wb_ao", [D, D], BF16)
    wb_o = dscr("wb_o", [D, D], BF16)
    wb_f1 = dscr("wb_f1", [D, 4096], BF16)
    wb_f2 = dscr("wb_f2", [4096, D], BF16)
    fd = dscr("fd", [8, JW], BF16)
    ygd = dscr("ygd", [L, 512], BF16)
    hnTd = dscr("hnTd", [128, 8, L], BF16)
    dbg = {}

    es = ExitStack()
    P = Prog(nc, es, n_slots=12)

    def MM(o, lhsT, rhs, start, stop, r, w, **kw):
        P.op("pe", lambda: nc.tensor.matmul(o, lhsT=lhsT, rhs=rhs, start=start, stop=stop, **kw), r=r, w=w)

    def TR(o, in_, idn, r, w):
        P.op("pe", lambda: nc.tensor.transpose(out=o, in_=in_, identity=idn), r=r, w=w)

    def ACT(o, in_, func, r, w, **kw):
        P.op("act", lambda: nc.scalar.activation(out=o, in_=in_, func=func, **kw), r=r, w=w)

    def veng(e):
        return nc.vector if e == "dve" else nc.gpsimd

    def TT(e, o, a, b, op, r, w):
        P.op(e, lambda: veng(e).tensor_tensor(out=o, in0=a, in1=b, op=op), r=r, w=w)

    def TS(e, o, a, s1, s2, op0, op1, r, w):
        if op1 is None:
            P.op(e, lambda: veng(e).tensor_scalar(out=o, in0=a, scalar1=s1, scalar2=None, op0=op0), r=r, w=w)
        else:
            P.op(e, lambda: veng(e).tensor_scalar(out=o, in0=a, scalar1=s1, scalar2=s2, op0=op0, op1=op1), r=r, w=w)

    def STT(o, a, s, b, op0, op1, r, w):
        P.op("dve", lambda: nc.vector.scalar_tensor_tensor(out=o, in0=a, scalar=s, in1=b, op0=op0, op1=op1), r=r, w=w)

    def CP(e, o, in_, r, w):
        if e == "act":
            P.op("act", lambda: nc.scalar.copy(out=o, in_=in_), r=r, w=w)
        else:
            P.op(e, lambda: veng(e).tensor_copy(out=o, in_=in_), r=r, w=w)

    def RECIP(o, in_, r, w):
        P.op("dve", lambda: nc.vector.reciprocal(out=o, in_=in_), r=r, w=w)

    def MEMSET(e, ap, val, w):
        P.op(e, lambda: veng(e).memset(ap, val), r=[], w=w)

    pb = [P.ps("pb%d" % i, [128, 512], F32) for i in range(8)]
    pbb = [pb[i][:].bitcast(BF16) for i in range(8)]
    PK = ["pb%d" % i for i in range(8)]

    ident = P.sb("ident", [128, 128], BF16)
    identf = P.sb("identf", [128, 128], F32)
    bones = P.sb("bones", [128, 128], BF16)
    mhalf = P.sb("mhalf", [128, 1], F32)
    epsb = P.sb("epsb", [128, 1], F32)
    MEMSET("pool", epsb[:], EPS, ["epsb"])
    MEMSET("pool", mhalf[:], -0.5, ["mhalf"])
    P.make_identity(ident, "ident")
    P.make_identity(identf, "identf")
    MEMSET("pool", bones[:], 0.0, ["bones"])
    MEMSET("pool", bones[0:64, 0:64], 1.0 / 64, ["bones"])
    MEMSET("pool", bones[64:128, 64:128], 1.0 / 64, ["bones"])

    gmix = P.sb("gmix", [128, 8], F32)
    gff = P.sb("gff", [128, 8], F32)
    gsub = P.sb("gsub", [128, 1], F32)
    qg = P.sb("qg", [128, 1], F32)
    kg = P.sb("kg", [128, 1], F32)
    lamt = P.sb("lamt", [128, 1], F32)
    farb = P.sb("farb", [128, 2, 8], F32)
    with nc.allow_non_contiguous_dma(reason="small param loads"):
        pass
    P.dma("sp", gmix[:], nmix.rearrange("o (k p) -> p (o k)", p=128), w=["gmix"])
    P.dma("sp", gff[:], nff.rearrange("o (k p) -> p (o k)", p=128), w=["gff"])
    P.dma("sp", gsub[:], subln_d.rearrange("o p -> p o"), w=["gsub"])
    TS("dve", gsub[:], gsub[:], 1.0 - LAM_INIT, None, ALU.mult, None, ["gsub"], ["gsub"])
    for hf in range(2):
        P.dma("sp", qg[64 * hf:64 * hf + 64, :], qn_d.rearrange("o p -> p o"), w=["qg"])
        P.dma("sp", kg[64 * hf:64 * hf + 64, :], kn_d.rearrange("o p -> p o"), w=["kg"])
    P.dma("sp", farb[:, 0, :], relb[15:16, :].broadcast_to((128, 8)), w=["farb"])
    P.dma("sp", farb[:, 1, :], relb[31:32, :].broadcast_to((128, 8)), w=["farb"])
    TT("dve", farb[:, 1, :], farb[:, 1, :], farb[:, 0, :], ALU.subtract, ["farb"], ["farb"])
    with ExitStack() as es0:
        P.es = es0
        l4 = P.sb("l4", [128, 4, 64], F32)
        lp = P.sb("lp", [128, 2, 64], F32)
        ls = P.sb("ls", [128, 2], F32)
        for i, a in enumerate([lq1, lk1, lq2, lk2]):
            P.dma("sp", l4[:, i, :], a.broadcast_to((128, 64)), w=["l4"])
        TT("dve", lp[:, 0, :], l4[:, 0, :], l4[:, 1, :], ALU.mult, ["l4"], ["lp"])
        TT("dve", lp[:, 1, :], l4[:, 2, :], l4[:, 3, :], ALU.mult, ["l4", "lp"], ["lp"])
        P.op("dve", lambda: nc.vector.reduce_sum(out=ls[:], in_=lp[:], axis=AX.X), r=["lp"], w=["ls"])
        ACT(ls[:], ls[:], AF.Exp, ["ls"], ["ls"])
        TT("dve", lamt[:], ls[:, 0:1], ls[:, 1:2], ALU.subtract, ["ls"], ["lamt"])
        TS("dve", lamt[:], lamt[:], LAM_INIT, None, ALU.add, None, ["lamt"], ["lamt"])
    P.fence()
    P.es = es
    if stage <= 0:
        return nc, P, es, dbg

    Gt = P.sb("Gt", [128, 8, GW], BF16)
    chunks1, chunks2 = [], []
    for (src, dst, K, N, gain) in [
        (w_in, wb_in, D, 5632, "mix"), (w_a, wb_a, 512, D, None), (w_b, wb_b, 512, D, None),
        (w_ao, wb_ao, D, D, "sub"), (w_o, wb_o, D, D, None),
        (w_f1, wb_f1, D, 4096, "ff"), (w_f2, wb_f2, 4096, D, None),
    ]:
        for kt in range(K // 128):
            for c0 in range(0, N, 1024):
                cw = min(1024, N - c0)
                g = None
                if gain == "mix":
                    g = (gmix[:, kt:kt + 1], "gmix")
                elif gain == "ff":
                    g = (gff[:, kt:kt + 1], "gff")
                elif gain == "sub":
                    g = (gsub[:, 0:1], "gsub")
                ch = (src[kt * 128:(kt + 1) * 128, c0:c0 + cw], dst[kt * 128:(kt + 1) * 128, c0:c0 + cw], cw, g, dst.tensor.name)
                (chunks1 if (src is w_in and c0 == 0) else chunks2).append(ch)
    Wc = {"chunks": chunks1, "i": 0, "loaded": -1, "wst": None, "wob": None, "NB": 3}

    def w_load(i):
        sc, d, cw, g, nm = Wc["chunks"][i]
        bi = i % Wc["NB"]
        P.dma("sp", Wc["wst"][bi][:, 0:cw], sc, w=["wst%d" % bi])
        Wc["loaded"] = i

    def w_step(engs):
        i = Wc["i"]
        if i >= len(Wc["chunks"]):
            return False
        if Wc["loaded"] < i:
            w_load(i)
        if i + 1 < len(Wc["chunks"]):
            w_load(i + 1)
        sc, d, cw, g, nm = Wc["chunks"][i]
        bi = i % Wc["NB"]
        e = engs[i % len(engs)]
        wst, wob = Wc["wst"], Wc["wob"]
        rk = ["wst%d" % bi] + ([g[1]] if g else [])
        if e == "act":
            if g:
                ACT(wob[bi][:, 0:cw], wst[bi][:, 0:cw], AF.Copy, rk, ["wob%d" % bi], scale=g[0])
            else:
                CP("act", wob[bi][:, 0:cw], wst[bi][:, 0:cw], rk, ["wob%d" % bi])
        else:
            if g:
                TS(e, wob[bi][:, 0:cw], wst[bi][:, 0:cw], g[0], None, ALU.mult, None, rk, ["wob%d" % bi])
            else:
                CP(e, wob[bi][:, 0:cw], wst[bi][:, 0:cw], rk, ["wob%d" % bi])
        P.dma("sp", d, wob[bi][:, 0:cw], r=["wob%d" % bi], w=[nm])
        Wc["i"] += 1
        return True

    with ExitStack() as esw:
        P.es = esw
        tab = P.sb("tab", [32, 8], F32)
        ohs = P.sb("ohs", [32, JW], F32)
        fsb = P.sb("fsb", [8, JW], BF16)
        cneg = P.sb("cneg", [8, 1], F32)
        P.dma("sp", tab[:], relb, w=["tab"])
        P.dma("sp", cneg[:], relb[15:16, :].rearrange("o h -> h o"), w=["cneg"])
        P.dma("sp", ohs[:], oh, w=["ohs"])
        for c0 in range(0, JW, 512):
            cw = min(512, JW - c0)
            MM(pb[0][0:8, 0:cw], tab[:], ohs[:, c0:c0 + cw], True, True, ["tab", "ohs"], [PK[0]])
            TS("dve", fsb[:, c0:c0 + cw], pb[0][0:8, 0:cw], cneg[:, 0:1], 8.0, ALU.subtract, ALU.mult, [PK[0], "cneg"], ["fsb"])
        P.dma("sp", fd, fsb[:], r=["fsb"], w=["fd"])
        Wc["wst"] = [P.sb("wst%d" % i, [128, 1024], F32) for i in range(3)]
        Wc["wob"] = [P.sb("wob%d" % i, [128, 1024], BF16) for i in range(3)]
        while w_step(["dve", "act"]):
            pass
    P.fence()
    P.es = es
    for k in range(128):
        src = bass.AP(fd.tensor, 127 - k, [[0, 1], [JW, 8], [1, GW]])
        P.dma("pool", Gt[k:k + 1, :, :], src, r=["fd"], w=["Gtrow%d" % k])
    P.op("pool", lambda: nc.gpsimd.nop(), r=["Gtrow%d" % k for k in range(128)], w=["Gt"])
    if stage <= 1:
        return nc, P, es, dbg

    hk = {"i": 0}

    def make_hnT(src_rows, npart, dst, dkey, xkeep=None):
        i = hk["i"]
        hk["i"] += 1
        b = i % 2
        xt = xts[b]
        xk = "xt%d" % b
        if xkeep is not None:
            xt, xk = xkeep
        P.dma("sp", xt[0:npart, :], src_rows, w=[xk])
        ACT(xn[b][0:npart, :], xt[0:npart, :], AF.Square, [xk], ["xn%d" % b, "xss%d" % b], accum_out=xss[b][0:npart, :])
        TS("dve", xss[b][0:npart, :], xss[b][0:npart, :], 1.0 / D, EPS, ALU.mult, ALU.add, ["xss%d" % b], ["xss%d" % b])
        ACT(xss[b][0:npart, :], xss[b][0:npart, :], AF.Sqrt, ["xss%d" % b], ["xss%d" % b])
        RECIP(xss[b][0:npart, :], xss[b][0:npart, :], ["xss%d" % b], ["xss%d" % b])
        TS("dve", xn[b][0:npart, :], xt[0:npart, :], xss[b][0:npart, 0:1], None, ALU.mult, None, [xk, "xss%d" % b], ["xn%d" % b])
        pbi = 6 + b
        pv = pbb[pbi].rearrange("p (k t) -> p k t", t=128)
        for k in range(8):
            TR(pv[:, k, 0:npart], xn[b][0:npart, k * 128:(k + 1) * 128], ident[0:npart, 0:npart], ["xn%d" % b, "ident"], [PK[pbi]])
        CP("dve" if b == 0 else "pool" if False else "dve", dst, pv[:, :, 0:npart], [PK[pbi]], [dkey])

    xts = [P.sb("xt%d" % i, [128, D], F32) for i in range(2)]
    xss = [P.sb("xss%d" % i, [128, 1], F32) for i in range(2)]
    xn = [P.sb("xn%d" % i, [128, D], BF16) for i in range(2)]

    PI = float(np.pi)
    es5 = ExitStack()
    P.es = es5
    U = P.sb("U", [128, 32, 2, 260], BF16)
    with ExitStack() as e1a:
        P.es = e1a
        hnTb = P.sb("hnTb", [128, 8, 2048], BF16)
        UU = P.sb("UU", [128, 32, 16, 16], BF16)
        wu = P.sb("wu", [128, 8, 512], BF16)
        P.dma("sp", wu[:], wb_in[:, 0:512].rearrange("(k p) n -> p k n", p=128), r=["wb_in"], w=["wu"])
        for sbi in range(3):
            if sbi < 2:
                for j in range(16):
                    r0 = 2048 * sbi + 128 * j
                    make_hnT(x[r0:r0 + 128, :], 128, hnTb[:, :, 128 * j:128 * j + 128], "hnTb")
                nrow = 128
                n0 = 1 + 128 * sbi
                P.dma("sp", hnTd[:, :, 16 + 2048 * sbi:16 + 2048 * (sbi + 1)], hnTb[:, :, 0:2048], r=["hnTb"], w=["hnTd%d" % sbi])
            else:
                make_hnT(meta[:, :], 16, hnTb[:, :, 0:16], "hnTb")
                nrow = 1
                n0 = 0
                P.dma("sp", hnTd[:, :, 0:16], hnTb[:, :, 0:16], r=["hnTb"], w=["hnTd2"])
            for tp in range(16):
                pbi = tp % 2
                for k in range(8):
                    MM(pb[pbi][0:nrow, :], hnTb[:, k, sl(tp, nrow, 16)], wu[:, k, :], k == 0, k == 7,
                       ["hnTb", "wu"], [PK[pbi]])
                CP("act" if tp % 2 else "dve", UU[0:nrow, :, tp, :],
                   pb[pbi][0:nrow, :].rearrange("p (g c) -> p g c", c=16), [PK[pbi]], ["UU"])
            for g0 in range(0, 32, 4):
                pbi = 2 + (g0 // 4) % 2
                pv = pbb[pbi].rearrange("p (a h n) -> p a h n", a=4, h=2)
                for a in range(4):
                    uflat = UU[0:nrow, g0 + a].rearrange("p t c -> p (t c)")
                    for hf in range(2):
                        TR(pv[:, a, hf, 0:nrow], uflat[:, 128 * hf:128 * hf + 128], ident[0:nrow, 0:nrow],
                           ["UU", "ident"], [PK[pbi]])
                CP("dve", U[:, g0:g0 + 4, :, n0:n0 + nrow], pv[:, :, :, 0:nrow], [PK[pbi]], ["U"])
    P.fence()
    P.es = es5
    if debug and stage == 2:
        dbg["U"] = nc.dram_tensor("dbg_U", [128, 32 * 2 * 260], BF16, kind="ExternalOutput").ap()
        P.dma("sp", dbg["U"], U[:].rearrange("p g h n -> p (g h n)"), r=["U"], w=["dbgU"])
        return nc, P, es, dbg

    SPW = {}
    for nm in ["PWA", "PWB", "PWZ", "PWY", "Bt", "Cc"]:
        SPW[nm] = (P.sb("S" + nm + "r", [128, 32, 16], BF16), P.sb("S" + nm + "i", [128, 32, 16], BF16))
    AR2 = P.sb("AR2", [128, 2, 32], F32)
    AI2 = P.sb("AI2", [128, 2, 32], F32)
    drep = P.sb("drep", [128, 32], F32)
    maskf = P.sb("maskf", [128, 2, 256], F32)
    maskb = P.sb("maskb", [128, 2, 256], F32)
    for j in range(8):
        P.dma("sp", drep[16 * j:16 * j + 16, :], s5d_d[0, :].rearrange("(g c) -> c g", c=16), w=["drep"])
    with ExitStack() as ept:
        P.es = ept

        def T(name, shape=(128, 64), dt=F32):
            return P.sb("t_" + name, list(shape), dt)

        PW = {}
        for nm in ["PWA", "PWB", "PWZ", "PWY"]:
            PW[nm] = (P.sb(nm + "r", [128, 64, 16], BF16), P.sb(nm + "i", [128, 64, 16], BF16))
        Btr = P.sb("Btr", [128, 64, 16], BF16)
        Bti = P.sb("Bti", [128, 64, 16], BF16)
        Ccr = P.sb("Ccr", [128, 64, 16], BF16)
        Cci = P.sb("Cci", [128, 64, 16], BF16)

        lre, lim, lst = T("lre"), T("lim"), T("lst")
        bre = T("bre", (128, 64, 16))
        bim = T("bim", (128, 64, 16))
        crow = [T("crow0", (128, 8, 2, 64)), T("crow1", (128, 8, 2, 64))]
        cre = T("cre", (128, 64, 16))
        cim = T("cim", (128, 64, 16))
        for hf in range(2):
            s_ = slice(64 * hf, 64 * hf + 64)
            P.dma("sp", lre[s_], lamre_d.rearrange("d g p -> p (d g)"), w=["lre"])
            P.dma("sp", lim[s_], lamim_d.rearrange("d g p -> p (d g)"), w=["lim"])
            P.dma("sp", lst[s_], lstep_d.broadcast_to((64, 64)), w=["lst"])
            P.dma("sp", bre[s_], bre_d.rearrange("d g p c -> p (d g) c"), w=["bre"])
            P.dma("sp", bim[s_], bim_d.rearrange("d g p c -> p (d g) c"), w=["bim"])
            P.dma("sp", crow[0][:, :, hf, :], cre_d.rearrange("(b r) p -> r b p", r=128), w=["crow0"])
            P.dma("sp", crow[1][:, :, hf, :], cim_d.rearrange("(b r) p -> r b p", r=128), w=["crow1"])
        for ci, cdst in enumerate([cre, cim]):
            for b in range(8):
                pbi = b % 2
                TR(pb[pbi][:, 0:128], crow[ci][:, b].rearrange("r u p -> r (u p)"), identf[:], ["crow%d" % ci, "identf"], [PK[pbi]])
                CP("dve", cdst[:, 8 * b:8 * b + 8, :], pb[pbi][:, 0:128].rearrange("p (a c) -> p a c", c=16), [PK[pbi]], ["c%d" % ci])
        CP("dve", Ccr[:], cre[:], ["c0"], ["Ccr"])
        CP("dve", Cci[:], cim[:], ["c1"], ["Cci"])
        dtt, tmp, mag, inv, th, kf, rr, m1, sn, cs = [T(n) for n in ["dtt", "tmp", "mag", "inv", "th", "kf", "rr", "m1", "sn", "cs"]]
        ki = T("ki", (128, 64), I32)
        are, aim, den, am1, cfr, cfi, t1, t2 = [T(n) for n in ["are", "aim", "den", "am1", "cfr", "cfi", "t1", "t2"]]
        K_ = lambda *a: list(a)
        ACT(dtt[:], lst[:], AF.Exp, ["lst"], ["dtt"])
        TT("dve", tmp[:], lre[:], dtt[:], ALU.mult, ["lre", "dtt"], ["tmp"])
        ACT(mag[:], tmp[:], AF.Exp, ["tmp"], ["mag"])
        ACT(inv[:], tmp[:], AF.Exp, ["tmp"], ["inv"], scale=-2.0)
        TT("dve", th[:], lim[:], dtt[:], ALU.mult, ["lim", "dtt"], ["th"])
        TS("dve", kf[:], th[:], 1.0 / (2 * PI), 0.5, ALU.mult, ALU.add, ["th"], ["kf"])
        CP("dve", ki[:], kf[:], ["kf"], ["ki"])
        CP("dve", kf[:], ki[:], ["ki"], ["kf"])
        STT(rr[:], kf[:], -2 * PI, th[:], ALU.mult, ALU.add, ["kf", "th"], ["rr"])

        def wrap(v, key):
            TS("dve", m1[:], v[:], PI, None, ALU.is_gt, None, [key], ["m1"])
            STT(v[:], m1[:], -2 * PI, v[:], ALU.mult, ALU.add, ["m1", key], [key])
            TS("dve", m1[:], v[:], -PI, None, ALU.is_lt, None, [key], ["m1"])
            STT(v[:], m1[:], 2 * PI, v[:], ALU.mult, ALU.add, ["m1", key], [key])

        PIc = 3.1415925
        wrap(rr, "rr")
        TS("dve", rr[:], rr[:], PIc, -PIc, ALU.min, ALU.max, ["rr"], ["rr"])
        ACT(sn[:], rr[:], AF.Sin, ["rr"], ["sn"])
        TS("dve", rr[:], rr[:], PI / 2, None, ALU.add, None, ["rr"], ["rr"])
        wrap(rr, "rr")
        TS("dve", rr[:], rr[:], PIc, -PIc, ALU.min, ALU.max, ["rr"], ["rr"])
        ACT(cs[:], rr[:], AF.Sin, ["rr"], ["cs"])
        TT("dve", are[:], mag[:], cs[:], ALU.mult, ["mag", "cs"], ["are"])
        TT("dve", aim[:], mag[:], sn[:], ALU.mult, ["mag", "sn"], ["aim"])
        TT("dve", den[:], lre[:], lre[:], ALU.mult, ["lre"], ["den"])
        TT("dve", t1[:], lim[:], lim[:], ALU.mult, ["lim"], ["t1"])
        TT("dve", den[:], den[:], t1[:], ALU.add, ["den", "t1"], ["den"])
        RECIP(den[:], den[:], ["den"], ["den"])
        TS("dve", am1[:], are[:], -1.0, None, ALU.add, None, ["are"], ["am1"])
        TT("dve", t1[:], am1[:], lre[:], ALU.mult, ["am1", "lre"], ["t1"])
        TT("dve", t2[:], aim[:], lim[:], ALU.mult, ["aim", "lim"], ["t2"])
        TT("dve", t1[:], t1[:], t2[:], ALU.add, ["t1", "t2"], ["t1"])
        TT("dve", cfr[:], t1[:], den[:], ALU.mult, ["t1", "den"], ["cfr"])
        TT("dve", t1[:], aim[:], lre[:], ALU.mult, ["aim", "lre"], ["t1"])
        TT("dve", t2[:], am1[:], lim[:], ALU.mult, ["am1", "lim"], ["t2"])
        TT("dve", t1[:], t1[:], t2[:], ALU.subtract, ["t1", "t2"], ["t1"])
        TT("dve", cfi[:], t1[:], den[:], ALU.mult, ["t1", "den"], ["cfi"])
        w1 = T("w1", (128, 64, 17))
        w2 = T("w2", (128, 64, 17))

        def cmul(o_r, o_i, a_r, a_i, b_r, b_i, shp, rk, wk, neg_im=False):
            n = shp[-1]
            v1 = w1[:, :shp[1], 0:n] if len(shp) == 3 else w1[:, :, 0]
            v2 = w2[:, :shp[1], 0:n] if len(shp) == 3 else w2[:, :, 0]
            TT("dve", v1, a_r, b_r, ALU.mult, rk, ["w1"])
            TT("dve", v2, a_i, b_i, ALU.mult, rk, ["w2"])
            TT("dve", o_r, v1, v2, ALU.subtract, ["w1", "w2"], wk)
            TT("dve", v1, a_r, b_i, ALU.mult, rk, ["w1"])
            TT("dve", v2, a_i, b_r, ALU.mult, rk, ["w2"])
            TT("dve", o_i, v1, v2, ALU.add, ["w1", "w2"], wk)

        bc16 = lambda t_: t_[:, :, None].broadcast_to((128, 64, 16))
        cmul(Btr[:], Bti[:], bc16(cfr), bc16(cfi), bre[:], bim[:], (128, 64, 16), ["cfr", "cfi", "bre", "bim"], ["Bt"])
        posr, posi = T("posr", (128, 64, 17)), T("posi", (128, 64, 17))
        negr, negi = T("negr", (128, 64, 17)), T("negi", (128, 64, 17))
        air, aii = T("air"), T("aii")
        TT("dve", air[:], are[:], inv[:], ALU.mult, ["are", "inv"], ["air"])
        TT("dve", aii[:], aim[:], inv[:], ALU.mult, ["aim", "inv"], ["aii"])
        TS("dve", aii[:], aii[:], -1.0, None, ALU.mult, None, ["aii"], ["aii"])
        for (pr, pi_, ar_, ai_, kk, akeys) in [(posr, posi, are, aim, "pos", ["are", "aim"]), (negr, negi, air, aii, "neg", ["air", "aii"])]:
            MEMSET("dve", pr[:, :, 0:1], 1.0, [kk])
            MEMSET("dve", pi_[:, :, 0:1], 0.0, [kk])
            CP("dve", pr[:, :, 1], ar_[:], akeys + [kk], [kk])
            CP("dve", pi_[:, :, 1], ai_[:], akeys + [kk], [kk])
            m = 1
            while m < 16:
                shp = (128, 64, m)
                cmul(pr[:, :, m + 1:2 * m + 1], pi_[:, :, m + 1:2 * m + 1],
                     pr[:, :, 1:m + 1], pi_[:, :, 1:m + 1],
                     pr[:, :, m:m + 1].broadcast_to(shp), pi_[:, :, m:m + 1].broadcast_to(shp),
                     shp, [kk], [kk])
                m *= 2
        lo, hi = slice(0, 32), slice(32, 64)
        for ri, (ps_, ng_) in enumerate([(posr, negr), (posi, negi)]):
            CP("dve", PW["PWA"][ri][:, lo, :], ng_[:, lo, 0:16], ["neg"], ["PWA"])
            CP("dve", PW["PWA"][ri][:, hi, :], ps_[:, hi, 0:16], ["pos"], ["PWA"])
            CP("dve", PW["PWB"][ri][:, lo, :], ps_[:, lo, 0:16], ["pos"], ["PWB"])
            CP("dve", PW["PWB"][ri][:, hi, :], ng_[:, hi, 0:16], ["neg"], ["PWB"])
            CP("dve", PW["PWZ"][ri][:, hi, :], ps_[:, hi, 0:16], ["pos"], ["PWZ"])
            CP("dve", PW["PWY"][ri][:, lo, :], ps_[:, lo, 1:17], ["pos"], ["PWY"])
        shp = (128, 32, 16)
        cmul(PW["PWZ"][0][:, lo, :], PW["PWZ"][1][:, lo, :], negr[:, lo, 0:16], negi[:, lo, 0:16],
             posr[:, lo, 15:16].broadcast_to(shp), posi[:, lo, 15:16].broadcast_to(shp), shp, ["pos", "neg"], ["PWZ"])
        cmul(PW["PWY"][0][:, hi, :], PW["PWY"][1][:, hi, :], negr[:, hi, 0:16], negi[:, hi, 0:16],
             posr[:, hi, 16:17].broadcast_to(shp), posi[:, hi, 16:17].broadcast_to(shp), shp, ["pos", "neg"], ["PWY"])
        for ri in range(2):
            CP("dve", AR2[0:64, ri, :], posr[0:64, lo, 16], ["pos"], ["AR2"])
            CP("dve", AR2[64:128, ri, :], posr[64:128, hi, 16], ["pos"], ["AR2"])
        CP("dve", AI2[0:64, 1, :], posi[0:64, lo, 16], ["pos"], ["AI2"])
        CP("dve", AI2[64:128, 1, :], posi[64:128, hi, 16], ["pos"], ["AI2"])
        TS("dve", AI2[:, 0, :], AI2[:, 1, :], -1.0, None, ALU.mult, None, ["AI2"], ["AI2"])
        qi = T("qi", (128, 1), I32)
        qf = T("qf", (128, 2))
        li_ = T("li_", (128, 256), I32)
        lf_ = T("lf_", (128, 256))
        P.op("pool", lambda: nc.gpsimd.iota(qi[:], pattern=[[0, 1]], base=0, channel_multiplier=1), w=["qi"])
        P.op("dve", lambda: nc.vector.tensor_scalar(out=qi[:], in0=qi[:], scalar1=4, scalar2=None, op0=ALU.arith_shift_right), r=["qi"], w=["qi"])
        CP("dve", qf[:, 0:1], qi[:], ["qi"], ["qf"])
        TS("dve", qf[:, 1:2], qf[:, 0:1], 8.0, None, ALU.add, None, ["qf"], ["qf"])
        P.op("pool", lambda: nc.gpsimd.iota(li_[:], pattern=[[1, 16], [0, 16]], base=0, channel_multiplier=0), w=["li_"])
        CP("dve", lf_[:], li_[:], ["li_"], ["lf_"])
        for hf in range(2):
            TS("dve", maskf[:, hf, :], lf_[:], qf[:, hf:hf + 1], None, ALU.is_ge, None, ["lf_", "qf"], ["maskf"])
            TS("dve", maskb[:, hf, :], lf_[:], qf[:, hf:hf + 1], None, ALU.is_le, None, ["lf_", "qf"], ["maskb"])
        for nm, (sr, si_), rk in [("PWA", PW["PWA"], ["PWA"]), ("PWB", PW["PWB"], ["PWB"]), ("PWZ", PW["PWZ"], ["PWZ"]),
                                  ("PWY", PW["PWY"], ["PWY"]), ("Bt", (Btr, Bti), ["Bt"]), ("Cc", (Ccr, Cci), ["Ccr", "Cci"])]:
            for ri, src in enumerate([sr, si_]):
                CP("dve", SPW[nm][ri][0:64, :, :], src[0:64, 0:32, :], rk, ["SPW"])
                CP("pool", SPW[nm][ri][64:128, :, :], src[64:128, 32:64, :], rk, ["SPW"])
        if debug and stage == 25:
            def DUMP(name, ap, dt, rk):
                shp = [ap.shape[0], int(np.prod(ap.shape[1:]))]
                dbg[name] = nc.dram_tensor("dbg_" + name, shp, dt, kind="ExternalOutput").ap()
                P.dma("sp", dbg[name], ap, r=rk, w=["dbg" + name])
            f2 = lambda t_: t_[:].rearrange("p a b -> p (a b)") if len(t_.shape) == 3 else t_[:]
            for nm, t_, dt_, rk in [("dtt", dtt, F32, ["dtt"]), ("mag", mag, F32, ["mag"]), ("inv", inv, F32, ["inv"]), ("th", th, F32, ["th"]),
                                    ("kf", kf, F32, ["kf"]), ("rr", rr, F32, ["rr"]), ("sn", sn, F32, ["sn"]), ("cs", cs, F32, ["cs"]),
                                    ("are", are, F32, ["are"]), ("aim", aim, F32, ["aim"]), ("cfr", cfr, F32, ["cfr"]), ("cfi", cfi, F32, ["cfi"]),
                                    ("posr", posr, F32, ["pos"]), ("posi", posi, F32, ["pos"]), ("negr", negr, F32, ["neg"]), ("negi", negi, F32, ["neg"]),
                                    ("Btr", Btr, BF16, ["Bt"]), ("Bti", Bti, BF16, ["Bt"]), ("Ccr", Ccr, BF16, ["Ccr"]), ("Cci", Cci, BF16, ["Cci"]),
                                    ("PWZr", PW["PWZ"][0], BF16, ["PWZ"]), ("PWYi", PW["PWY"][1], BF16, ["PWY"]),
                                    ("AR2", AR2, F32, ["AR2"]), ("AI2", AI2, F32, ["AI2"]), ("maskf", maskf, F32, ["maskf"]), ("maskb", maskb, F32, ["maskb"]),
                                    ("drep", drep, F32, ["drep"])]:
                DUMP(nm, f2(t_), dt_, rk)
            return nc, P, es, dbg
    P.fence()
    P.es = es5

    gtp = []
    for par in range(2):
        gtp.append({nm: P.sb("%s_%d" % (nm, par), [128, 256], BF16) for nm in ["g1r", "g1i", "g2r", "g2i", "g3r", "g3i"]})
    gw = [P.sb("gw%d" % i, [128, 16, 16], F32) for i in range(8)]

    def outer(o_r, o_i, pw, Y, g, e, okeys, neg_im=False):
        shp = (128, 16, 16)
        Xr = pw[0][:, g, :][:, :, None].broadcast_to(shp)
        Xi = pw[1][:, g, :][:, :, None].broadcast_to(shp)
        yr = Y[0][:, g, :][:, None, :].broadcast_to(shp)
        yi = Y[1][:, g, :][:, None, :].broadcast_to(shp)
        o_r4 = o_r[:].rearrange("p (j c) -> p j c", c=16)
        o_i4 = o_i[:].rearrange("p (j c) -> p j c", c=16)
        o0 = 0 if e == "dve" else 4
        a, b, c_, d_ = gw[o0], gw[o0 + 1], gw[o0 + 2], gw[o0 + 3]
        ka, kb, kc, kd = ["gw%d" % (o0 + i) for i in range(4)]
        rk = ["SPW"]
        TT(e, a[:], Xr, yr, ALU.mult, rk, [ka])
        TT(e, b[:], Xi, yi, ALU.mult, rk, [kb])
        TT(e, c_[:], Xr, yi, ALU.mult, rk, [kc])
        TT(e, d_[:], Xi, yr, ALU.mult, rk, [kd])
        TT(e, o_r4, a[:], b[:], ALU.subtract, [ka, kb], okeys)
        if neg_im:
            P.op(e, lambda: veng(e).scalar_tensor_tensor(out=o_i4, in0=c_[:], scalar=-1.0, in1=d_[:], op0=ALU.mult, op1=ALU.subtract),
                 r=[kc, kd], w=okeys) if e == "dve" else (
                TS(e, c_[:], c_[:], -1.0, None, ALU.mult, None, [kc], [kc]),
                TT(e, o_i4, c_[:], d_[:], ALU.subtract, [kc, kd], okeys))
        else:
            TT(e, o_i4, c_[:], d_[:], ALU.add, [kc, kd], okeys)

    Sb = P.sb("Sb", [128, 2, 32, 258], BF16)
    Wc.update(chunks=chunks2, i=0, loaded=-1)
    Wc["wst"] = [P.sb("wst2_%d" % i, [128, 1024], F32) for i in range(3)]
    Wc["wob"] = [P.sb("wob2_%d" % i, [128, 1024], BF16) for i in range(3)]
    esz = ExitStack()
    P.es = esz
    Zs = P.sb("Zs", [128, 2, 32, 258], F32)
    WZT = [P.sb("WZT%d" % i, [128, 4, 128], BF16) for i in range(2)]
    for g in range(32):
        w_step(["act"])
        par = g % 2
        gt = gtp[par]
        e = "dve" if g % 2 == 0 else "pool"
        outer(gt["g1r"], gt["g1i"], SPW["PWZ"], SPW["Bt"], g, e, ["g1_%d" % par])
        pv = pbb[2 + par].rearrange("p (a m) -> p a m", m=128)
        for ri in range(2):
            src = gt["g1r"] if ri == 0 else gt["g1i"]
            for hf in range(2):
                TR(pv[:, ri * 2 + hf, :], src[:, 128 * hf:128 * hf + 128], ident[:], ["g1_%d" % par, "ident"], [PK[2 + par]])
        CP("act", WZT[par][:], pv[:, 0:4, :], [PK[2 + par]], ["WZT%d" % par])
        for ri in range(2):
            bk = 4 + 2 * par + ri
            for hf in range(2):
                MM(pb[bk][:, 0:257], WZT[par][:, ri * 2 + hf, :], U[:, g, hf, 0:257], hf == 0, hf == 1, ["WZT%d" % par, "U"], [PK[bk]])
            CP("dve" if ri == 0 else "act", Zs[:, ri, g, 0:257], pb[bk][:, 0:257], [PK[bk]], ["ZsF", "ZsB"])
    rt = [P.sb("rt%d" % i, [128, 2, 32], F32) for i in range(4)]
    fw, bw = slice(0, 64), slice(64, 128)
    for i in range(1, 257):
        if i % 3 == 0:
            w_step(["act"])
        n = i
        TT("dve", rt[0][fw], AR2[fw], Zs[fw, :, :, n - 1], ALU.mult, ["AR2", "ZsF"], ["rt0"])
        TT("dve", rt[1][fw], AI2[fw], Zs[fw, ::-1, :, n - 1], ALU.mult, ["AI2", "ZsF"], ["rt1"])
        TT("dve", rt[0][fw], rt[0][fw], rt[1][fw], ALU.add, ["rt0", "rt1"], ["rt0"])
        TT("dve", Zs[fw, :, :, n], Zs[fw, :, :, n], rt[0][fw], ALU.add, ["ZsF", "rt0"], ["ZsF"])
        n = 256 - i
        TT("pool", rt[2][bw], AR2[bw], Zs[bw, :, :, n + 1], ALU.mult, ["AR2", "ZsB"], ["rt2"])
        TT("pool", rt[3][bw], AI2[bw], Zs[bw, ::-1, :, n + 1], ALU.mult, ["AI2", "ZsB"], ["rt3"])
        TT("pool", rt[2][bw], rt[2][bw], rt[3][bw], ALU.add, ["rt2", "rt3"], ["rt2"])
        TT("pool", Zs[bw, :, :, n], Zs[bw, :, :, n], rt[2][bw], ALU.add, ["ZsB", "rt2"], ["ZsB"])
    MEMSET("pool", Sb[fw, :, :, 0:1], 0.0, ["Sb"])
    MEMSET("pool", Sb[bw, :, :, 256:258], 0.0, ["Sb"])
    CP("dve", Sb[fw, :, :, 1:258], Zs[fw, :, :, 0:257], ["ZsF"], ["Sb"])
    CP("act", Sb[bw, :, :, 0:256], Zs[bw, :, :, 1:257], ["ZsB"], ["Sb"])
    if debug and stage == 3:
        dbg["Zs"] = nc.dram_tensor("dbg_Zs", [128, 2 * 32 * 258], F32, kind="ExternalOutput").ap()
        P.dma("sp", dbg["Zs"], Zs[:].rearrange("p r g n -> p (r g n)"), r=["ZsF", "ZsB"], w=["dbgZ"])
        return nc, P, es, dbg
    P.fence()
    esz.close()
    P.es = es5
    YY = P.sb("YY", [128, 2, 16, 32, 16], BF16)
    Toep2 = [P.sb("Toep%d" % i, [128, 2, 256], BF16) for i in range(2)]
    diag2 = [P.sb("diag%d" % i, [128, 128], BF16) for i in range(2)]
    tq2 = [[P.sb("tq%d_%d" % (i, j), [128, 256], F32) for j in range(2)] for i in range(2)]
    ysq2 = [P.sb("ysq%d" % i, [128, 257], F32) for i in range(2)]
    yin2 = [P.sb("yin%d" % i, [128, 257], F32) for i in range(2)]
    ygh2 = [P.sb("ygh%d" % i, [128, 260], BF16) for i in range(2)]
    def p2_tables(g):
        par = g % 2
        gt = gtp[par]
        outer(gt["g1r"], gt["g1i"], SPW["PWA"], SPW["Bt"], g, "dve", ["g1_%d" % par])
        outer(gt["g2r"], gt["g2i"], SPW["PWB"], SPW["Cc"], g, "dve", ["g2_%d" % par], neg_im=True)
        outer(gt["g3r"], gt["g3i"], SPW["PWY"], SPW["Cc"], g, "pool", ["g3_%d" % par], neg_im=True)

    p2_tables(0)
    for g in range(32):
        for _ in range(2):
            w_step(["act"])
        par = g % 2
        gt = gtp[par]
        Toep, diag, tq = Toep2[par], diag2[par], tq2[par]
        kg1, kg2, kg3, kT, kD = "g1_%d" % par, "g2_%d" % par, "g3_%d" % par, "Toep%d" % par, "diag%d" % par
        if g + 1 < 32:
            p2_tables(g + 1)
        if debug and stage == 31:
            return nc, P, es, dbg
        for hf in range(2):
            for d in range(2):
                bd = 2 * par + d
                ds_ = slice(64 * d, 64 * d + 64)
                MM(pb[bd][:, 0:256], gt["g1r"][ds_, 128 * hf:128 * hf + 128], gt["g2r"][ds_, :], True, False, [kg1, kg2], [PK[bd]])
                MM(pb[bd][:, 0:256], gt["g1i"][ds_, 128 * hf:128 * hf + 128], gt["g2i"][ds_, :], False, True, [kg1, kg2], [PK[bd]])
            TT("dve", tq[0][:], maskf[:, hf, :], pb[2 * par][:, 0:256], ALU.mult, ["maskf", PK[2 * par]], ["tq0_%d" % par])
            TT("dve", tq[1][:], maskb[:, hf, :], pb[2 * par + 1][:, 0:256], ALU.mult, ["maskb", PK[2 * par + 1]], ["tq1_%d" % par])
            TT("pool", Toep[:, hf, :], tq[0][:], tq[1][:], ALU.add, ["tq0_%d" % par, "tq1_%d" % par], [kT])
        if debug and stage == 32:
            return nc, P, es, dbg
        ACT(diag[:], ident[:], AF.Copy, ["ident", "drep"], [kD], scale=drep[:, g:g + 1])
        if debug and stage == 33:
            return nc, P, es, dbg
        for ho in range(2):
            bk = 4 + ho
            cs_ = slice(128 * ho, 128 * ho + 128)
            MM(pb[bk][:, 0:257], Toep[:, 0, cs_], U[:, g, 0, 0:257], True, False, [kT, "U"], [PK[bk]])
            MM(pb[bk][:, 0:257], Toep[:, 1, cs_], U[:, g, 1, 0:257], False, False, [kT, "U"], [PK[bk]])
            MM(pb[bk][:, 0:257], diag[:], U[:, g, ho, 0:257], False, False, [kD, "U"], [PK[bk]])
            MM(pb[bk][:, 0:257], gt["g3r"][:, cs_], Sb[:, 0, g, 0:257], False, False, [kg3, "Sb"], [PK[bk]])
            MM(pb[bk][:, 0:257], gt["g3i"][:, cs_], Sb[:, 1, g, 0:257], False, True, [kg3, "Sb"], [PK[bk]])
            if debug and stage == 34:
                return nc, P, es, dbg
            yp = pb[bk][:, 0:257]
            ysq, yin, ygh = ysq2[ho], yin2[ho], ygh2[ho]
            ksq, kin, kgh = "ysq%d" % ho, "yin%d" % ho, "ygh%d" % ho
            ACT(ysq[:], yp, AF.Square, [PK[bk]], [ksq])
            TS("dve", ysq[:], ysq[:], 0.044715, 1.0, ALU.mult, ALU.add, [ksq], [ksq])
            TT("dve", yin[:], ysq[:], yp, ALU.mult, [ksq, PK[bk]], [kin])
            ACT(yin[:], yin[:], AF.Sigmoid, [kin], [kin], scale=1.5957691216057308)
            TT("dve", ygh[:, 1:258], yin[:], yp, ALU.mult, [kin, PK[bk]], [kgh])
            if debug and stage == 35:
                return nc, P, es, dbg
            for sbi in range(2):
                pbi = 6 + sbi
                TR(pbb[pbi][:, 0:128], ygh[:, 2 + 128 * sbi:130 + 128 * sbi], ident[:], [kgh, "ident"], [PK[pbi]])
                CP("act", YY[:, sbi, 8 * ho:8 * ho + 8, g, :], pbb[pbi][:, 0:128].rearrange("p (t c) -> p t c", c=16),
                   [PK[pbi]], ["YY"])
    while w_step(["act", "dve"]):
        pass
    if debug and stage == 36:
        return nc, P, es, dbg
    for sbi in range(2):
        r0 = 16 + 2048 * sbi
        P.dma("sp", ygd[r0:r0 + 2048, :].rearrange("(n t) c -> n (t c)", t=16),
              YY[:, sbi].rearrange("p t g c -> p (t g c)"), r=["YY"], w=["ygd"])
    P.fence()
    es5.close()
    P.es = es
    if debug and stage == 4:
        dbg["yg"] = nc.dram_tensor("dbg_yg", [L, 512], BF16, kind="ExternalOutput").ap()
        P.dma("sp", dbg["yg"][16:L, :], ygd[16:L, :], r=["ygd"], w=["dbgyg"])
        return nc, P, es, dbg
    if debug and stage == 5:
        dbg["Gt"] = nc.dram_tensor("dbg_Gt", [128, 8 * GW], BF16, kind="ExternalOutput").ap()
        P.dma("sp", dbg["Gt"], Gt[:].rearrange("p h m -> p (h m)"), r=["Gt"], w=["dbgGt"])
        return nc, P, es, dbg

    eskv = ExitStack()
    P.es = eskv
    KT = P.sb("KT", [128, 8, L], BF16)
    Vp = P.sb("Vp", [128, 33, 8, 129], BF16)
    MEMSET("pool", Vp[:, :, :, 128:129], 1.0, ["Vp"])
    OFF_Q, OFF_K, OFF_V, OFF_GS, OFF_GA = 512, 1536, 2560, 3584, 4608

    def wchunk(dst, src2d, nk, r, w):
        P.dma("sp", dst, src2d.rearrange("(k p) n -> p k n", p=128), r=r, w=w)

    def qk_proj(dst, wch, wkey, hnT_, hkey, ntok, gain, sqq, rstd, dkey, par=0, split=None,
                sqkeys=("sqq0", "sqq1", "sqq2", "sqq3"), stage=0):
        pa, pq = par % 4, 4 + par % 4
        sq_, rs_ = sqq[par % 4], rstd[par % 2]
        ks, kr = sqkeys[par % 4], "rstdq%d" % (par % 2)
        if stage in (0, 1):
            for k in range(8):
                MM(pb[pa][:, 0:ntok], wch[:, k, :], hnT_[:, k, 0:ntok], k == 0, k == 7, [wkey, hkey], [PK[pa]])
            ACT(sq_[:, 0:ntok], pb[pa][:, 0:ntok], AF.Square, [PK[pa]], [ks])
        if stage in (0, 2):
            MM(pb[pq][:, 0:ntok], bones[:], sq_[:, 0:ntok], True, True, ["bones", ks], [PK[pq]])
            ACT(rs_[:, 0:ntok], pb[pq][:, 0:ntok], AF.Ln, [PK[pq], "epsb"], [kr], bias=epsb[:, 0:1])
            ACT(rs_[:, 0:ntok], rs_[:, 0:ntok], AF.Exp, [kr], [kr], scale=-0.5)
            if split is None:
                STT(dst, pb[pa][:, 0:ntok], gain[:, 0:1], rs_[:, 0:ntok], ALU.mult, ALU.mult, [PK[pa], kr, "qg", "kg"], [dkey])
            else:
                for hf_, d_ in enumerate(split):
                    sl_ = slice(64 * hf_, 64 * hf_ + 64)
                    STT(d_, pb[pa][sl_, 0:ntok], gain[sl_, 0:1], rs_[sl_, 0:ntok], ALU.mult, ALU.mult,
                        [PK[pa], kr, "qg", "kg", "QTz"], [dkey])

    with ExitStack() as e1b:
        P.es = e1b
        hnTs = [P.sb("hnT1_%d" % i, [128, 8, 512], BF16) for i in range(2)]
        wk = [P.sb("wk%d" % i, [128, 8, 128], BF16) for i in range(4)]
        wv = [P.sb("wv%d" % i, [128, 8, 512], BF16) for i in range(2)]
        sqq = [P.sb("sqq1_%d" % i, [128, 512], BF16) for i in range(4)]
        rstdq = [P.sb("rstdq1_%d" % i, [128, 512], F32) for i in range(2)]
        wi = 0
        for s_ in range(9):
            hnT = hnTs[s_ % 2]
            hk1 = "hnT1_%d" % (s_ % 2)
            ntok, pos0 = (512, 16 + 512 * s_) if s_ < 8 else (16, 0)
            P.dma("sp", hnT[:, :, 0:ntok], hnTd[:, :, pos0:pos0 + ntok], r=["hnTd0", "hnTd1", "hnTd2"], w=[hk1])
            def kp(h, st):
                qk_proj(KT[:, h, pos0:pos0 + ntok], wk[h % 4], "wk%d" % (h % 4), hnT, hk1, ntok, kg, sqq, rstdq, "KT",
                        par=h, stage=st)

            for h in range(10):
                if h < 8:
                    wchunk(wk[h % 4][:], wb_in[:, OFF_K + 128 * h:OFF_K + 128 * h + 128], 8, ["wb_in"], ["wk%d" % (h % 4)])
                    kp(h, 1)
                if h >= 2:
                    kp(h - 2, 2)
            for half in range(2):
                wchunk(wv[half][:], wb_in[:, OFF_V + 512 * half:OFF_V + 512 * half + 512], 8, ["wb_in"], ["wv%d" % half])
                ntile = 4 if s_ < 8 else 1
                for j in range(ntile):
                    npt = 128 if s_ < 8 else 16
                    kt = 1 + 4 * s_ + j if s_ < 8 else 0
                    bk = (half * 4 + j) % 4
                    for k in range(8):
                        MM(pb[bk][0:npt, :], hnT[:, k, 128 * j:128 * j + npt], wv[half][:, k, :], k == 0, k == 7,
                           [hk1, "wv%d" % half], [PK[bk]])
                    CP("act" if j % 2 else "dve", Vp[0:npt, kt, 4 * half:4 * half + 4, 0:128],
                       pb[bk][0:npt, :].rearrange("p (h e) -> p h e", e=128), [PK[bk]], ["Vp"])
    P.fence()
    P.es = eskv
    if debug and stage == 6:
        dbg["KT"] = nc.dram_tensor("dbg_KT", [128, 8 * L], BF16, kind="ExternalOutput").ap()
        P.dma("sp", dbg["KT"], KT[:].rearrange("p h m -> p (h m)"), r=["KT"], w=["dbgKT"])
        dbg["Vp"] = nc.dram_tensor("dbg_Vp", [128, 33 * 8 * 129], BF16, kind="ExternalOutput").ap()
        P.dma("sp", dbg["Vp"], Vp[:].rearrange("p a h m -> p (a h m)"), r=["Vp"], w=["dbgVp"])
        return nc, P, es, dbg

    nsb = 8 if stage >= 8 else 1
    for s_ in range(nsb):
        pos0 = 16 + 512 * s_
        esA = ExitStack()
        P.es = esA
        hnT = P.sb("hnT3", [128, 8, 512], BF16)
        onT = P.sb("onT", [128, 8, 512], BF16)
        P.dma("sp", hnT[:], hnTd[:, :, pos0:pos0 + 512], r=["hnTd0", "hnTd1", "hnTd2"], w=["hnT3"])
        with ExitStack() as eat:
            P.es = eat
            QT = P.sb("QT", [128, 8, 2, 512], BF16)
            MEMSET("pool", QT[:].rearrange("p h t q -> p (h t q)"), 0.0, ["QTz"])
            wq = [P.sb("wq%d" % i, [128, 8, 128], BF16) for i in range(2)]
            rstdq = [P.sb("rstdq3_%d" % i, [128, 512], F32) for i in range(2)]
            pt = [P.sb("pt%d" % i, [128, 512], BF16) for i in range(4)]
            sqq = pt[0:4]
            Osb = [P.sb("Osb0", [128, 4, 258], F32)] * 2
            dn = P.sb("dn", [128, 2], F32)
            otmp = P.sb("otmp", [128, 128], F32)
            of = P.sb("of", [128, 128], F32)
            osq = P.sb("osq", [128, 128], F32)
            oss = P.sb("oss", [128, 1], F32)
            onb = P.sb("onb", [128, 2, 4, 128], BF16)
            def qp(h, st):
                b = h % 2
                qk_proj(None, wq[b], "wq%d" % b, hnT, "hnT3", 512, qg, sqq, rstdq, "QT%d" % h, par=h,
                        split=(QT[0:64, h, 0, :], QT[64:128, h, 1, :]), sqkeys=("pt0", "pt1", "pt2", "pt3"), stage=st)

            for h in range(8):
                wchunk(wq[h % 2][:], wb_in[:, OFF_Q + 128 * h:OFF_Q + 128 * h + 128], 8, ["wb_in"], ["wq%d" % (h % 2)])
                qp(h, 0)
            units = [(h, t, kt) for h in range(8) for t in range(2) for kt in range(33)]

            def geom(kt):
                return (16, 0) if kt == 0 else (128, 16 + 128 * (kt - 1))

            def emit_S(i):
                h, t, kt = units[i]
                nk, kp0 = geom(kt)
                ts_ = slice(64 * t, 64 * t + 64)
                cls = []
                for qt in range(4):
                    delta = kp0 - (pos0 + 128 * qt)
                    cls.append("pos" if delta >= 218 else ("neg" if delta <= -90 - nk else "near"))
                sbk = (0, 1, 6)[i % 3]
                pti = i % 4
                S = pb[sbk]
                mixed = len(set(cls)) > 1 or cls[0] == "near"
                MM(S[0:nk, :], KT[:, h, kp0:kp0 + nk], QT[:, h, t, :], True, not mixed,
                   ["KT", "QT%d" % h, "QTz"], [PK[sbk]])
                if mixed:
                    for qt in range(4):
                        if cls[qt] == "near":
                            off = (pos0 + 128 * qt) - kp0 + GD
                        elif cls[qt] == "pos":
                            off = 0
                        else:
                            off = 570
                        MM(S[0:nk, 128 * qt:128 * qt + 128], ident[0:nk, 0:nk], Gt[0:nk, h, off:off + 128], False,
                           qt == 3, ["ident", "Gt"], [PK[sbk]])
                    ACT(pt[pti][0:nk, :], S[0:nk, :], AF.Exp, [PK[sbk]], ["pt%d" % pti], scale=0.125)
                elif cls[0] == "pos":
                    ACT(pt[pti][0:nk, :], S[0:nk, :], AF.Exp, [PK[sbk], "farb"], ["pt%d" % pti], scale=0.125,
                        bias=farb[0:nk, 1, h:h + 1])
                else:
                    ACT(pt[pti][0:nk, :], S[0:nk, :], AF.Exp, [PK[sbk]], ["pt%d" % pti], scale=0.125)

            def emit_PV(i):
                h, t, kt = units[i]
                nk, kp0 = geom(kt)
                pti = i % 4
                for qt in range(4):
                    ob = 2 + 2 * t + qt // 2
                    c0 = 129 * (qt % 2)
                    MM(pb[ob][:, c0:c0 + 129], pt[pti][0:nk, 128 * qt:128 * qt + 128], Vp[0:nk, kt, h, :],
                       kt == 0 and qt % 2 == 0, kt == 32, ["pt%d" % pti, "Vp"], [PK[ob]], skip_group_check=True)

            def epilogue(h):
                ob_ = Osb[h % 2]
                ok = "Osb0"
                for j in range(4):
                    CP("dve", ob_[:, j, :], pb[2 + j][:, 0:258], [PK[2 + j]], [ok])
                for qt in range(4):
                    c0 = 129 * (qt % 2)
                    O0 = ob_[:, qt // 2, :]
                    O1 = ob_[:, 2 + qt // 2, :]
                    CP("dve", dn[:, 0:1], O0[:, c0 + 128:c0 + 129], [ok], ["dn"])
                    CP("dve", dn[:, 1:2], O1[:, c0 + 128:c0 + 129], [ok], ["dn"])
                    RECIP(dn[:], dn[:], ["dn"], ["dn"])
                    TT("dve", dn[:, 1:2], dn[:, 1:2], lamt[:], ALU.mult, ["dn", "lamt"], ["dn"])
                    TS("dve", otmp[:], O1[:, c0:c0 + 128], dn[:, 1:2], None, ALU.mult, None, [ok, "dn"], ["otmp"])
                    STT(of[:], O0[:, c0:c0 + 128], dn[:, 0:1], otmp[:], ALU.mult, ALU.subtract, [ok, "dn", "otmp"], ["of"])
                    TT("dve", osq[:], of[:], of[:], ALU.mult, ["of"], ["osq"])
                    P.op("dve", lambda: nc.vector.reduce_sum(out=oss[:], in_=osq[:], axis=AX.X), r=["osq"], w=["oss"])
                    TS("dve", oss[:], oss[:], 1.0 / 128, EPS, ALU.mult, ALU.add, ["oss"], ["oss"])
                    TT("pool", oss[:], oss[:], mhalf[:], ALU.pow, ["oss", "mhalf"], ["oss"])
                    TS("dve", onb[:, h % 2, qt, :], of[:], oss[:, 0:1], None, ALU.mult, None, ["of", "oss"], ["onb%d" % (h % 2)])

            def tr_heads(h):
                pv = pbb[7].rearrange("p (q t) -> p q t", t=128)
                for qt in range(4):
                    TR(pv[:, qt, :], onb[:, h % 2, qt, :], ident[:], ["onb%d" % (h % 2), "ident"], [PK[7]])
                CP("dve", onT[:, h, :].rearrange("p (q t) -> p q t", t=128), pv[:, 0:4, :], [PK[7]], ["onT"])

            n_u = len(units)
            for i in range(n_u + 2):
                if i < n_u:
                    emit_S(i)
                if i >= 2:
                    emit_PV(i - 2)
                    h_, t_, kt_ = units[i - 2]
                    if t_ == 1 and kt_ == 32:
                        epilogue(h_)
                        if h_ >= 1:
                            tr_heads(h_ - 1)
            tr_heads(7)
        P.fence()
        P.es = esA
        if debug and stage == 7:
            dbg["onT"] = nc.dram_tensor("dbg_onT", [128, 8 * 512], BF16, kind="ExternalOutput").ap()
            P.dma("sp", dbg["onT"], onT[:].rearrange("p h m -> p (h m)"), r=["onT"], w=["dbgonT"])
            return nc, P, es, dbg
        emm = ExitStack()
        P.es = emm
        mT = P.sb("mT", [128, 8, 512], BF16)
        with ExitStack() as emg:
            P.es = emg
            ygT = P.sb("ygT", [128, 4, 512], BF16)
            ygk = [P.sb("ygk%d" % i, [128, 512], BF16) for i in range(2)]
            ring8 = [P.sb("r8_%d" % i, [128, 8, 128], BF16) for i in range(3)]
            ring4 = [P.sb("r4_%d" % i, [128, 4, 128], BF16) for i in range(2)]
            gas = P.sb("gas", [128, 512], BF16)
            gss = P.sb("gss", [128, 512], BF16)
            sbs = P.sb("sbs", [128, 512], BF16)
            tg = P.sb("tg", [128, 512], BF16)
            m1 = P.sb("m1", [128, 512], BF16)
            m2 = P.sb("m2", [128, 512], BF16)
            r8i = 0
            r4i = 0
            for f in range(8):
                fs = slice(128 * f, 128 * f + 128)
                w8 = []
                for src in [wb_ao[:, fs], wb_in[:, OFF_GA + 128 * f:OFF_GA + 128 * f + 128], wb_in[:, OFF_GS + 128 * f:OFF_GS + 128 * f + 128]]:
                    i_ = r8i % 3
                    r8i += 1
                    wchunk(ring8[i_][:], src, 8, ["wb_ao", "wb_in"], ["r8_%d" % i_])
                    w8.append((ring8[i_], "r8_%d" % i_))
                w4 = []
                for src in [wb_a[:, fs], wb_b[:, fs]]:
                    i_ = r4i % 2
                    r4i += 1
                    wchunk(ring4[i_][:], src, 4, ["wb_a", "wb_b"], ["r4_%d" % i_])
                    w4.append((ring4[i_], "r4_%d" % i_))
                for k in range(8):
                    MM(pb[0][:, :], w8[0][0][:, k, :], onT[:, k, :], k == 0, k == 7, [w8[0][1], "onT"], [PK[0]])
                for k in range(8):
                    MM(pb[1][:, :], w8[1][0][:, k, :], hnT[:, k, :], k == 0, k == 7, [w8[1][1], "hnT3"], [PK[1]])
                for k in range(8):
                    MM(pb[4][:, :], w8[2][0][:, k, :], hnT[:, k, :], k == 0, k == 7, [w8[2][1], "hnT3"], [PK[4]])
                if f == 0:
                    for j in range(4):
                        b = j % 2
                        P.dma("sp", ygk[b][:], ygd[pos0 + 128 * j:pos0 + 128 * j + 128, :], r=["ygd"], w=["ygk%d" % b])
                        pv = pbb[6 + b].rearrange("p (c t) -> p c t", t=128)
                        for c in range(4):
                            TR(pv[:, c, :], ygk[b][:, 128 * c:128 * c + 128], ident[:], ["ygk%d" % b, "ident"], [PK[6 + b]])
                        CP("dve", ygT[:, :, 128 * j:128 * j + 128], pv[:, 0:4, :], [PK[6 + b]], ["ygT"])
                for k in range(4):
                    MM(pb[2][:, :], w4[0][0][:, k, :], ygT[:, k, :], k == 0, k == 3, [w4[0][1], "ygT"], [PK[2]])
                for k in range(4):
                    MM(pb[3][:, :], w4[1][0][:, k, :], ygT[:, k, :], k == 0, k == 3, [w4[1][1], "ygT"], [PK[3]])
                ACT(gas[:], pb[1][:, :], AF.Sigmoid, [PK[1]], ["gas"])
                ACT(gss[:], pb[4][:, :], AF.Sigmoid, [PK[4]], ["gss"])
                ACT(sbs[:], pb[3][:, :], AF.Sigmoid, [PK[3]], ["sbs"])
                TT("dve", m1[:], gas[:], pb[0][:, :], ALU.mult, ["gas", PK[0]], ["m1"])
                TT("pool", tg[:], sbs[:], gss[:], ALU.mult, ["sbs", "gss"], ["tg"])
                TT("dve", m2[:], tg[:], pb[2][:, :], ALU.mult, ["tg", PK[2]], ["m2"])
                TT("pool", mT[:, f, :], m1[:], m2[:], ALU.add, ["m1", "m2"], ["mT"])
        P.fence()
        P.es = emm
        with ExitStack() as emo:
            P.es = emo
            wo = [P.sb("wo%d" % i, [128, 8, 512], BF16) for i in range(2)]
            xh = [P.sb("xh%d" % i, [128, 512], F32) for i in range(3)]
            xi_ = 0
            for c in range(2):
                wchunk(wo[c][:], wb_o[:, 512 * c:512 * c + 512], 8, ["wb_o"], ["wo%d" % c])
                for j in range(4):
                    r0 = 512 * s_ + 128 * j
                    xb = xi_ % 3
                    xi_ += 1
                    P.dma("sp", xh[xb][:], x[r0:r0 + 128, 512 * c:512 * c + 512], w=["xh%d" % xb])
                    bk = 2 + j
                    for k in range(8):
                        MM(pb[bk][:, :], mT[:, k, 128 * j:128 * j + 128], wo[c][:, k, :], k == 0, k == 7,
                           ["mT", "wo%d" % c], [PK[bk]])
                    TT("dve", xh[xb][:], xh[xb][:], pb[bk][:, :], ALU.add, ["xh%d" % xb, PK[bk]], ["xh%d" % xb])
                    P.dma("sp", out[r0:r0 + 128, 512 * c:512 * c + 512], xh[xb][:], r=["xh%d" % xb], w=["outd"])
        P.fence()
        emm.close()
        esA.close()
    P.fence()
    eskv.close()
    P.es = es
    if debug and stage in (8, 1008):
        return nc, P, es, dbg

    with ExitStack() as eff:
        P.es = eff
        h2t = [P.sb("h2t%d" % i, [128, 4, D], F32) for i in range(2)]
        hn2T = [P.sb("hn2T%d" % i, [128, 8, 512], BF16) for i in range(2)]
        xn4 = [P.sb("xn4_%d" % i, [128, D], BF16) for i in range(4)]
        ss4 = [P.sb("ss4_%d" % i, [128, 1], F32) for i in range(4)]
        fT = P.sb("fT", [128, 32, 512], BF16)
        w1 = [P.sb("w1_%d" % i, [128, 8, 512], BF16) for i in range(2)]
        w2 = [P.sb("w2_%d" % i, [128, 8, 512], BF16) for i in range(2)]
        rl = [P.sb("rl%d" % i, [128, 512], BF16) for i in range(2)]
        w2i = 0

        def prep_a(s_):
            p = s_ % 2
            for j in range(4):
                r0 = 512 * s_ + 128 * j
                hk_ = "h2t%d_%d" % (p, j)
                P.dma("sp", h2t[p][:, j, :], out[r0:r0 + 128, :], w=[hk_])
                ACT(xn4[j][:], h2t[p][:, j, :], AF.Square, [hk_], ["xn4_%d" % j, "ss4_%d" % j], accum_out=ss4[j][:])
                TS("dve", ss4[j][:], ss4[j][:], 1.0 / D, EPS, ALU.mult, ALU.add, ["ss4_%d" % j], ["ss4_%d" % j])
                TT("pool", ss4[j][:], ss4[j][:], mhalf[:], ALU.pow, ["ss4_%d" % j, "mhalf"], ["ss4_%d" % j])
                TS("dve", xn4[j][:], h2t[p][:, j, :], ss4[j][:, 0:1], None, ALU.mult, None, [hk_, "ss4_%d" % j], ["xn4_%d" % j])

        def prep_b(s_):
            p = s_ % 2
            for j in range(4):
                pbi = 6 + j % 2
                pv = pbb[pbi].rearrange("p (k t) -> p k t", t=128)
                for k in range(8):
                    TR(pv[:, k, :], xn4[j][:, k * 128:(k + 1) * 128], ident[:], ["xn4_%d" % j, "ident"], [PK[pbi]])
                CP("dve", hn2T[p][:, :, 128 * j:128 * j + 128], pv[:, :, :], [PK[pbi]], ["hn2T%d" % p])

        prep_a(0)
        prep_b(0)
        for s_ in range(8):
            p = s_ % 2
            if s_ + 1 < 8:
                prep_a(s_ + 1)
            for ft in range(32):
                b3 = (ft // 4) % 2
                if ft % 4 == 0:
                    wchunk(w1[b3][:], wb_f1[:, 128 * ft:128 * ft + 512], 8, ["wb_f1"], ["w1_%d" % b3])
                bk = ft % 2
                fo = 128 * (ft % 4)
                for k in range(8):
                    MM(pb[bk][:, :], w1[b3][:, k, fo:fo + 128], hn2T[p][:, k, :], k == 0, k == 7, ["w1_%d" % b3, "hn2T%d" % p], [PK[bk]])
                ACT(rl[bk][:], pb[bk][:, :], AF.Relu, [PK[bk]], ["rl%d" % bk])
                TT("pool", fT[:, ft, :], rl[bk][:], rl[bk][:], ALU.mult, ["rl%d" % bk], ["fT"])
            if s_ + 1 < 8:
                prep_b(s_ + 1)
            for half in range(2):
                for kg in range(4):
                    wb_ = w2i % 2
                    w2i += 1
                    wchunk(w2[wb_][:], wb_f2[1024 * kg:1024 * kg + 1024, 512 * half:512 * half + 512], 8, ["wb_f2"], ["w2_%d" % wb_])
                    for j in range(4):
                        for k8 in range(8):
                            MM(pb[2 + j][:, :], fT[:, 8 * kg + k8, 128 * j:128 * j + 128], w2[wb_][:, k8, :],
                               kg == 0 and k8 == 0, kg == 3 and k8 == 7, ["fT", "w2_%d" % wb_], [PK[2 + j]])
                for j in range(4):
                    hk_ = "h2t%d_%d" % (p, j)
                    TT("dve", h2t[p][:, j, 512 * half:512 * half + 512], h2t[p][:, j, 512 * half:512 * half + 512], pb[2 + j][:, :],
                       ALU.add, [hk_, PK[2 + j]], [hk_])
            for j in range(4):
                r0 = 512 * s_ + 128 * j
                P.dma("sp", out[r0:r0 + 128, :], h2t[p][:, j, :], r=["h2t%d_%d" % (p, j)], w=["outd"])
    P.fence()
    P.es = es
    return nc, P, es, dbg


def _in_maps(inputs):
    f = lambda a: np.ascontiguousarray(np.asarray(a, dtype=np.float32))
    oh = _onehot_const()
    common = {
        "meta": f(inputs["meta_tokens"]), "relb": f(inputs["rel_bias_table"]), "oh": oh,
        "nmix": f(inputs["norm_mix"]).reshape(1, D), "w_in": f(inputs["w_in"])[0],
        "lamre": f(inputs["s5_lambda_re"])[0], "lamim": f(inputs["s5_lambda_im"])[0],
        "lstep": f(inputs["s5_log_step"]).reshape(1, 64),
        "bre": f(inputs["s5_b_re"])[0], "bim": f(inputs["s5_b_im"])[0],
        "cre": f(inputs["s5_c_re"]).reshape(1024, 64), "cim": f(inputs["s5_c_im"]).reshape(1024, 64),
        "s5d": f(inputs["s5_d"]).reshape(1, 512), "w_a": f(inputs["w_glu_a"])[0], "w_b": f(inputs["w_glu_b"])[0],
        "qn": f(inputs["q_norm"]).reshape(1, 64), "kn": f(inputs["k_norm"]).reshape(1, 64),
        "lq1": f(inputs["lambda_q1"]).reshape(1, 64), "lk1": f(inputs["lambda_k1"]).reshape(1, 64),
        "lq2": f(inputs["lambda_q2"]).reshape(1, 64), "lk2": f(inputs["lambda_k2"]).reshape(1, 64),
        "subln": f(inputs["attn_subln"]).reshape(1, 128), "w_ao": f(inputs["w_attn_out"])[0],
        "w_o": f(inputs["w_o"])[0], "nff": f(inputs["norm_ff"]).reshape(1, D),
        "w_f1": f(inputs["w_ff1"])[0], "w_f2": f(inputs["w_ff2"])[0],
    }
    xs = f(inputs["x"])
    return [dict(common, x=xs[b]) for b in range(xs.shape[0])]


def _run(inputs, stage=99, debug=False, cores=None, out_keys=("outd",)):
    nc, P, es, dbg = build_program(stage=stage, debug=debug)
    keys = list(out_keys) if not debug else [k for k in P.last_w if k.startswith("dbg") or k == "outd"]
    with nc.allow_non_contiguous_dma(reason="small strided parameter loads"):
        st = P.finish(keys)
    maps = _in_maps(inputs)
    if cores is not None:
        maps = maps[:cores]
    res = run_bass_kernel_spmd(nc, maps, core_ids=list(range(len(maps))))
    if not debug:
        es.close()
    return res, st


def kernel(**inputs):
    res, _ = _run(inputs)
    return np.stack([np.asarray(r["out"], dtype=np.float32) for r in res.results], axis=0)
```
